# Optimizing a Trainium2 kernel written in Bass

```python
import math
import jax, jax.numpy as jnp
from jax import lax
import numpy as np

D_MODEL = 1024
BATCH = 16
SEQ = 256
DEPTH = 1
DEC_BATCH = 4
DEC_SEQ = 1024
PAST_LEN = 256

GRID_W = 64
HEAD_DIM = 64
N_Q_HEADS = 8
N_KV_HEADS = 2
GQA_GROUP = N_Q_HEADS // N_KV_HEADS
Q_W = N_Q_HEADS * HEAD_DIM
KV_W = N_KV_HEADS * HEAD_DIM
LRU_W = 512
LRU_BLOCKS = 8
LRU_BW = LRU_W // LRU_BLOCKS
LRU_C = 8.0
CONV_W = 4
MIX_W = Q_W + LRU_W
IN_W = Q_W + 2 * KV_W + 2 * LRU_W
D_FF = 4 * D_MODEL
WINDOW = 128
BLK = 128
ROT_HALF = HEAD_DIM // 2
N_FREQ = ROT_HALF // 2
ROPE_THETA = 10000.0
NORM_EPS = 1e-6
NEG_INF = -1e30

kernel_name = "hybrid_diffusion_prefix_gqa_rglru_step"


def rmsnorm(x, g):
    xf = x.astype(jnp.float32)
    y = xf * lax.rsqrt(jnp.mean(xf * xf, axis=-1, keepdims=True) + NORM_EPS)
    return (y * g.astype(jnp.float32)).astype(x.dtype)


def modulation(cvec, mod_w, mod_b):
    m = jax.nn.silu(cvec) @ mod_w + mod_b
    return [t[:, None, :] for t in jnp.split(m, 6, axis=-1)]


def rope_2d(x):
    T = x.shape[1]
    rows = T // GRID_W
    row = jnp.repeat(jnp.arange(rows), GRID_W)
    col = jnp.tile(jnp.arange(GRID_W), rows)
    inv = ROPE_THETA ** (-jnp.arange(N_FREQ, dtype=jnp.float32) / N_FREQ)
    bshape = (1, T) + (1,) * (x.ndim - 3) + (N_FREQ,)

    def rot(xa, pos):
        ang = pos.astype(jnp.float32)[:, None] * inv
        cos = jnp.cos(ang).reshape(bshape).astype(x.dtype)
        sin = jnp.sin(ang).reshape(bshape).astype(x.dtype)
        x1, x2 = xa[..., :N_FREQ], xa[..., N_FREQ:]
        return jnp.concatenate([x1 * cos - x2 * sin, x2 * cos + x1 * sin], axis=-1)

    return jnp.concatenate([rot(x[..., :ROT_HALF], row), rot(x[..., ROT_HALF:], col)], axis=-1)


def sink_logits(sink, shape):
    s = sink.astype(jnp.float32).reshape(1, N_KV_HEADS, GQA_GROUP, 1, 1)
    return jnp.broadcast_to(s, shape[:-1] + (1,))


def context_attention(q, k, v, sink):
    B, S = q.shape[0], q.shape[1]
    nb = S // BLK
    scale = 1.0 / math.sqrt(HEAD_DIM)
    qb = q.reshape(B, nb, BLK, N_KV_HEADS, GQA_GROUP, HEAD_DIM).swapaxes(0, 1)

    def one_block(qi):
        s = jnp.einsum('bqkgd,bskd->bkgqs', qi, k).astype(jnp.float32) * scale
        logits = jnp.concatenate([s, sink_logits(sink, s.shape)], axis=-1)
        p = jax.nn.softmax(logits, axis=-1)[..., :S]
        return jnp.einsum('bkgqs,bskd->bqkgd', p.astype(v.dtype), v)

    o = lax.map(one_block, qb).swapaxes(0, 1)
    return o.reshape(B, S, Q_W)


def latent_attention(q, k, v, k_ctx, v_ctx, sink):
    B, T = q.shape[0], q.shape[1]
    C = k_ctx.shape[1]
    nb = T // BLK
    scale = 1.0 / math.sqrt(HEAD_DIM)
    pad = ((0, 0), (BLK, BLK), (0, 0), (0, 0))
    kp = jnp.pad(k, pad)
    vp = jnp.pad(v, pad)
    qb = q.reshape(B, nb, BLK, N_KV_HEADS, GQA_GROUP, HEAD_DIM).swapaxes(0, 1)
    q_off = jnp.arange(BLK)
    k_off = jnp.arange(3 * BLK)

    def one_block(args):
        qi, i = args
        start = i * BLK
        kl = lax.dynamic_slice_in_dim(kp, start, 3 * BLK, axis=1)
        vl = lax.dynamic_slice_in_dim(vp, start, 3 * BLK, axis=1)
        q_idx = start + q_off
        k_idx = start - BLK + k_off
        valid = ((k_idx[None, :] >= 0) & (k_idx[None, :] < T)
                 & (jnp.abs(q_idx[:, None] - k_idx[None, :]) <= WINDOW))
        s_loc = jnp.einsum('bqkgd,bskd->bkgqs', qi, kl).astype(jnp.float32) * scale
        s_loc = jnp.where(valid, s_loc, NEG_INF)
        s_ctx = jnp.einsum('bqkgd,bskd->bkgqs', qi, k_ctx).astype(jnp.float32) * scale
        logits = jnp.concatenate([s_loc, s_ctx, sink_logits(sink, s_loc.shape)], axis=-1)
        p = jax.nn.softmax(logits, axis=-1)
        p_loc = p[..., :3 * BLK].astype(v.dtype)
        p_ctx = p[..., 3 * BLK:3 * BLK + C].astype(v.dtype)
        return (jnp.einsum('bkgqs,bskd->bqkgd', p_loc, vl)
                + jnp.einsum('bkgqs,bskd->bqkgd', p_ctx, v_ctx))

    o = lax.map(one_block, (qb, jnp.arange(nb))).swapaxes(0, 1)
    return o.reshape(B, T, Q_W)


def centred_conv(x, w, b):
    T = x.shape[1]
    xp = jnp.pad(x, ((0, 0), (2, 1), (0, 0)))
    out = b
    for j in range(CONV_W):
        out = out + w[j] * xp[:, j:j + T]
    return out


def rglru_coeffs(xc, w_r, b_r, w_i, b_i, lam):
    B, T = xc.shape[0], xc.shape[1]
    xb = xc.reshape(B, T, LRU_BLOCKS, LRU_BW)
    r = jax.nn.sigmoid(jnp.einsum('btnc,ncd->btnd', xb, w_r).reshape(B, T, LRU_W) + b_r)
    ig = jax.nn.sigmoid(jnp.einsum('btnc,ncd->btnd', xb, w_i).reshape(B, T, LRU_W) + b_i)
    log_a = LRU_C * r.astype(jnp.float32) * jax.nn.log_sigmoid(lam.astype(jnp.float32))
    a = jnp.exp(log_a)
    u = jnp.sqrt(-jnp.expm1(2.0 * log_a)) * (ig * xc).astype(jnp.float32)
    return a, u


def lru_scan(a, u, h0, reverse):
    def step(h, au):
        a_t, u_t = au
        h = a_t * h + u_t
        return h, h
    h_last, hs = lax.scan(step, h0, (a.swapaxes(0, 1), u.swapaxes(0, 1)), reverse=reverse)
    return hs.swapaxes(0, 1), h_last


def trunk_layer(x, mods, l, w_in, w_out, attn_sink, conv_w, conv_b,
                lru_w_r, lru_b_r, lru_w_i, lru_b_i, lru_lam,
                g_pre_mix, g_post_mix, g_pre_mlp, g_post_mlp, mlp_w_up, mlp_w_down,
                ctx_k=None, ctx_v=None, ctx_state=None):
    latent = ctx_k is not None
    shift1, scale1, gate1, shift2, scale2, gate2 = mods
    B, T = x.shape[0], x.shape[1]

    h = rmsnorm(x, g_pre_mix[l]) * (1.0 + scale1) + shift1
    z = h @ w_in[l]
    q, k, v, lx, lg = jnp.split(z, [Q_W, Q_W + KV_W, Q_W + 2 * KV_W, Q_W + 2 * KV_W + LRU_W], axis=-1)
    q = q.reshape(B, T, N_KV_HEADS, GQA_GROUP, HEAD_DIM)
    k = k.reshape(B, T, N_KV_HEADS, HEAD_DIM)
    v = v.reshape(B, T, N_KV_HEADS, HEAD_DIM)

    if latent:
        attn = latent_attention(rope_2d(q), rope_2d(k), v, ctx_k, ctx_v, attn_sink[l])
    else:
        attn = context_attention(q, k, v, attn_sink[l])

    xc = centred_conv(lx, conv_w[l], conv_b[l])
    a_f, u_f = rglru_coeffs(xc, lru_w_r[l, 0], lru_b_r[l, 0], lru_w_i[l, 0], lru_b_i[l, 0], lru_lam[l, 0])
    a_b, u_b = rglru_coeffs(xc, lru_w_r[l, 1], lru_b_r[l, 1], lru_w_i[l, 1], lru_b_i[l, 1], lru_lam[l, 1])
    if latent:
        h0_f = ctx_state[:, 0].astype(jnp.float32)
        h0_b = ctx_state[:, 1].astype(jnp.float32)
    else:
        h0_f = jnp.zeros((B, LRU_W), jnp.float32)
        h0_b = jnp.zeros((B, LRU_W), jnp.float32)
    hs_f, hl_f = lru_scan(a_f, u_f, h0_f, reverse=False)
    hs_b, hl_b = lru_scan(a_b, u_b, h0_b, reverse=True)
    lru = (hs_f + hs_b).astype(x.dtype) * jax.nn.gelu(lg)

    mix = jnp.concatenate([attn, lru], axis=-1) @ w_out[l]
    x = x + gate1 * rmsnorm(mix, g_post_mix[l])

    h2 = rmsnorm(x, g_pre_mlp[l]) * (1.0 + scale2) + shift2
    f = jnp.square(jax.nn.relu(h2 @ mlp_w_up[l])) @ mlp_w_down[l]
    x = x + gate2 * rmsnorm(f, g_post_mlp[l])
    return x, k, v, jnp.stack([hl_f, hl_b], axis=1)


def setup_inputs(seed: int = 0) -> dict:
    key = jax.random.key(seed)
    ks = jax.random.split(key, 32)
    n = jax.random.normal
    f32 = jnp.float32
    u = jax.random.uniform(ks[20], (DEPTH, 2, LRU_W), f32, minval=0.9, maxval=0.999)
    s = u ** (1.0 / LRU_C)
    lam = jnp.log(s) - jnp.log1p(-s)
    return {
        "x_prompt": n(ks[0], (BATCH, SEQ, D_MODEL), f32),
        "x_sample": n(ks[1], (DEC_BATCH, DEC_SEQ, D_MODEL), f32),
        "c": n(ks[2], (DEC_BATCH, D_MODEL), f32),
        "cache_k": n(ks[3], (DEC_BATCH, DEPTH, PAST_LEN, N_KV_HEADS, HEAD_DIM), f32),
        "cache_v": n(ks[4], (DEC_BATCH, DEPTH, PAST_LEN, N_KV_HEADS, HEAD_DIM), f32),
        "state_lru": n(ks[5], (DEC_BATCH, DEPTH, 2, LRU_W), f32),
        "c_ctx": n(ks[6], (D_MODEL,), f32),
        "mod_w": n(ks[7], (DEPTH, D_MODEL, 6 * D_MODEL), f32) * (0.5 * D_MODEL ** -0.5),
        "mod_b": n(ks[8], (DEPTH, 6 * D_MODEL), f32) * 0.02,
        "g_pre_mix": 1.0 + 0.05 * n(ks[9], (DEPTH, D_MODEL), f32),
        "g_post_mix": 1.0 + 0.05 * n(ks[10], (DEPTH, D_MODEL), f32),
        "g_pre_mlp": 1.0 + 0.05 * n(ks[11], (DEPTH, D_MODEL), f32),
        "g_post_mlp": 1.0 + 0.05 * n(ks[12], (DEPTH, D_MODEL), f32),
        "w_in": n(ks[13], (DEPTH, D_MODEL, IN_W), f32) * D_MODEL ** -0.5,
        "w_out": n(ks[14], (DEPTH, MIX_W, D_MODEL), f32) * MIX_W ** -0.5,
        "attn_sink": 0.5 * n(ks[15], (DEPTH, N_Q_HEADS), f32),
        "conv_w": n(ks[16], (DEPTH, CONV_W, LRU_W), f32) * CONV_W ** -0.5,
        "conv_b": 0.02 * n(ks[17], (DEPTH, LRU_W), f32),
        "lru_w_r": n(ks[18], (DEPTH, 2, LRU_BLOCKS, LRU_BW, LRU_BW), f32) * LRU_BW ** -0.5,
        "lru_b_r": 0.02 * n(ks[19], (DEPTH, 2, LRU_W), f32),
        "lru_w_i": n(ks[21], (DEPTH, 2, LRU_BLOCKS, LRU_BW, LRU_BW), f32) * LRU_BW ** -0.5,
        "lru_b_i": 0.02 * n(ks[22], (DEPTH, 2, LRU_W), f32),
        "lru_lam": lam,
        "mlp_w_up": n(ks[23], (DEPTH, D_MODEL, D_FF), f32) * D_MODEL ** -0.5,
        "mlp_w_down": n(ks[24], (DEPTH, D_FF, D_MODEL), f32) * D_FF ** -0.5,
    }


def reference(x_prompt, x_sample, c, cache_k, cache_v, state_lru, c_ctx,
              mod_w, mod_b, g_pre_mix, g_post_mix, g_pre_mlp, g_post_mlp,
              w_in, w_out, attn_sink, conv_w, conv_b,
              lru_w_r, lru_b_r, lru_w_i, lru_b_i, lru_lam, mlp_w_up, mlp_w_down):
    weights = (w_in, w_out, attn_sink, conv_w, conv_b, lru_w_r, lru_b_r, lru_w_i, lru_b_i,
               lru_lam, g_pre_mix, g_post_mix, g_pre_mlp, g_post_mlp, mlp_w_up, mlp_w_down)

    xp = x_prompt
    ks_new, vs_new, st_new = [], [], []
    for l in range(DEPTH):
        mods = modulation(c_ctx[None, :], mod_w[l], mod_b[l])
        xp, k_l, v_l, s_l = trunk_layer(xp, mods, l, *weights)
        ks_new.append(k_l)
        vs_new.append(v_l)
        st_new.append(s_l)
    new_k = jnp.stack(ks_new, axis=1)
    new_v = jnp.stack(vs_new, axis=1)
    new_state_lru = jnp.stack(st_new, axis=1)

    xs = x_sample
    for l in range(DEPTH):
        mods = modulation(c, mod_w[l], mod_b[l])
        xs, _, _, _ = trunk_layer(xs, mods, l, *weights,
                                  ctx_k=cache_k[:, l], ctx_v=cache_v[:, l],
                                  ctx_state=state_lru[:, l])

    return (xp, xs, new_k, new_v, new_state_lru)
```

```python
import numpy as np
import concourse.bass as bass
import concourse.mybir as mybir
from concourse.bass_utils import run_bass_kernel_spmd

F32 = mybir.dt.float32
BF16 = mybir.dt.bfloat16
AF = mybir.ActivationFunctionType
ALU = mybir.AluOpType

ENGS = ("pe", "act", "dve", "pool", "sp")


class Tile:
    __slots__ = ("name", "w", "r", "dsem", "dcnt", "excl")

    def __init__(self, name):
        self.name = name
        self.excl = False
        self.w = None
        self.r = []
        self.dsem = None
        self.dcnt = 0


class Prog:
    def __init__(self, nc, strict_same=True):
        self.nc = nc
        self.ops = {e: [] for e in ENGS}
        self.cnt = {e: 0 for e in ENGS}
        self.sem = {}
        self.seen = {e: {} for e in ENGS}
        self.strict_same = strict_same
        self.stack = []
        self.out_tokens = []
        for e in ("pe", "act", "dve", "pool"):
            self.sem[e] = self._newsem("e_" + e)

    def _newsem(self, name):
        cm = self.nc.semaphore(name)
        s = cm.__enter__()
        self.stack.append(cm)
        return s

    def tile(self, name):
        return Tile(name)

    def alias(self, new, olds):
        for o in olds:
            if o.w is not None:
                new.r.append(o.w)
            new.r.extend(o.r)

    def _waits(self, eng, reads, writes):
        toks = []
        for t in reads:
            if t.w is not None:
                toks.append(t.w)
        for t in writes:
            if t.w is not None:
                toks.append(t.w)
            toks.extend(t.r)
        need = {}
        for (s, v, src) in toks:
            if src == eng and (eng in ("pe", "sp") or not self.strict_same):
                continue
            k = id(s)
            if self.seen[eng].get(k, 0) >= v:
                continue
            if k not in need or need[k][1] < v:
                need[k] = (s, v)
        for k, (s, v) in need.items():
            self.seen[eng][k] = v
            self.ops[eng].append(lambda e, s=s, v=v: e.wait_ge(s, v))

    def op(self, eng, fn, reads=(), writes=(), inc=True):
        ex = [t for t in reads if t.excl]
        if ex:
            reads = [t for t in reads if not t.excl]
            writes = list(writes) + ex
        self._waits(eng, reads, writes)
        s = self.sem[eng]
        tok = (s, self.cnt[eng] + 1, eng)
        if inc:
            self.cnt[eng] += 1
            self.ops[eng].append(lambda e, fn=fn, s=s: fn(e).then_inc(s, 1))
        else:
            self.ops[eng].append(lambda e, fn=fn: fn(e))
        for t in reads:
            t.r.append(tok)
        for t in writes:
            t.w = tok
            t.r = []
        return tok

    def dma(self, eng, out, in_, reads=(), writes=(), key=None, is_output=False, **kw):
        self._waits(eng, reads, writes)
        kt = key if key is not None else (writes[0] if writes else reads[0])
        if kt.dsem is None:
            kt.dsem = self._newsem("d_" + kt.name)
        kt.dcnt += 16
        s = kt.dsem
        tok = (s, kt.dcnt, "dma")
        self.ops[eng].append(
            lambda e, out=out, in_=in_, s=s, kw=kw: e.dma_start(out=out, in_=in_, **kw).then_inc(s, 16))
        for t in reads:
            t.r.append(tok)
        for t in writes:
            t.w = tok
            t.r = []
        if is_output:
            self.out_tokens.append(tok)
        return tok

    def finish(self):
        need = {}
        for (s, v, src) in self.out_tokens:
            k = id(s)
            if k not in need or need[k][1] < v:
                need[k] = (s, v)
        for k, (s, v) in need.items():
            self.ops["sp"].append(lambda e, s=s, v=v: e.wait_ge(s, v))

    def emit(self):
        nc = self.nc
        ops = self.ops
        with nc.Block() as block:
            @block.tensor
            def _(e):
                for f in ops["pe"]:
                    f(e)

            @block.scalar
            def _(e):
                for f in ops["act"]:
                    f(e)

            @block.vector
            def _(e):
                for f in ops["dve"]:
                    f(e)

            @block.gpsimd
            def _(e):
                for f in ops["pool"]:
                    f(e)

            @block.sync
            def _(e):
                for f in ops["sp"]:
                    f(e)


D = 1024
NT = 1024
NTT = 8
DFF = 4096
NEXT = 1920
C_LX = [0, 256, 512, 768]
C_LG = [128, 384, 640, 896]
C_KV = 1024
C_Q = 1280
C_K = 1792
NVEC = 64
V_CONVW = 0
V_CONVB = 16
V_BR = 20
V_BI = 28
V_LAM = 36
V_H0 = 44
V_FLAG = 52
V_CTXB = 53
V_C1 = 54
V_HC1 = 64
V_HBR = 72
V_HBI = 80
NVEC2 = 96


class _Stop(Exception):
    pass


def build(debug=False, stop=None):
    nc = bass.Bass("TRN2", target_bir_lowering=False)
    P = Prog(nc)
    try:
        _build_body(nc, P, debug, stop)
    except _Stop:
        pass
    P.finish()
    P.emit()
    return nc


def _build_body(nc, P, debug, stop):
    def chk(n):
        if stop == n:
            raise _Stop()

    def din(name, shape, dt=F32):
        return nc.dram_tensor(name, list(shape), dt, kind="ExternalInput").ap()

    def dout(name, shape, dt=F32):
        return nc.dram_tensor(name, list(shape), dt, kind="ExternalOutput").ap()

    x_d = din("x", [NT, D])
    cvec_d = din("cvecT", [128, 8])
    modw_d = din("mod_w", [D, 6 * D])
    modb_d = din("mod_b", [1, 6 * D])
    gvec_d = din("gvec", [4, D])
    win_d = din("w_in_ext", [D, NEXT])
    wout_d = din("w_out_p", [D, D])
    wup_d = din("w_up", [D, DFF])
    wdn_d = din("w_down", [DFF, D])
    ck_d = din("ck", [256, 128])
    cv_d = din("cv", [256, 128])
    vecs_d = din("vecs", [128, NVEC])
    sink_d = din("sinkmat", [128, 4])
    ident_d = din("ident", [128, 128])
    perm_d = din("perm", [128, 128])
    ropec_d = din("rope_c", [128, NT])
    ropes_d = din("rope_s", [128, NT])
    mask_d = din("masks", [128, 16 * 128])
    gatew_d = din("gatew", [128, 16 * 128])

    y_d = dout("y", [NT, D])
    nk_d = dout("new_k", [NT, 128])
    nv_d = dout("new_v", [NT, 128])
    nst_d = dout("new_state_t", [32, 128])
    x1s_d = nc.dram_tensor("x1_scratch", [NT, D], F32).ap()

    cms = []

    def sb(name, shape, dt):
        cm = nc.sbuf_tensor(name, list(shape), dt)
        cms.append(cm)
        return cm.__enter__()

    def psum(name):
        cm = nc.psum_tensor(name, [128, 1024], F32)
        cms.append(cm)
        return cm.__enter__()

    def tt(eng, out, in0, in1, op, reads, writes):
        return P.op(eng, lambda e: e.tensor_tensor(out=out, in0=in0, in1=in1, op=op), reads, writes)

    def ts(eng, out, in0, s1, op0, reads, writes, s2=None, op1=None):
        if op1 is None:
            return P.op(eng, lambda e: e.tensor_scalar(out=out, in0=in0, scalar1=s1, scalar2=None, op0=op0), reads, writes)
        return P.op(eng, lambda e: e.tensor_scalar(out=out, in0=in0, scalar1=s1, scalar2=s2, op0=op0, op1=op1), reads, writes)

    def stt(out, in0, scalar, in1, op0, op1, reads, writes):
        return P.op("dve", lambda e: e.scalar_tensor_tensor(out=out, in0=in0, scalar=scalar, in1=in1, op0=op0, op1=op1), reads, writes)

    def act(out, in_, func, reads, writes, scale=1.0, bias=0.0, accum_out=None):
        if accum_out is None:
            return P.op("act", lambda e: e.activation(out=out, in_=in_, func=func, scale=scale, bias=bias), reads, writes)
        return P.op("act", lambda e: e.activation(out=out, in_=in_, func=func, scale=scale, bias=bias, accum_out=accum_out), reads, writes)

    def mm(out, lhsT, rhs, start, stop_, reads, writes, inc=True):
        return P.op("pe", lambda e: e.matmul(out, lhsT=lhsT, rhs=rhs, start=start, stop=stop_), reads, writes, inc=inc)

    def trp(out, in_, idn, reads, writes, inc=True):
        return P.op("pe", lambda e: e.transpose(out=out, in_=in_, identity=idn), reads, writes, inc=inc)

    def cp(eng, out, in_, reads, writes):
        return P.op(eng, lambda e: e.tensor_copy(out=out, in_=in_), reads, writes)

    def memset(eng, ap, val, writes):
        return P.op(eng, lambda e: e.memset(ap, val), (), writes)

    def recip(out, in_, reads, writes):
        return P.op("dve", lambda e: e.reciprocal(out=out, in_=in_), reads, writes)

    def scan(out, d0, d1, init, reads, writes):
        return P.op("dve", lambda e: e.tensor_tensor_scan(out=out, data0=d0, data1=d1, initial=init, op0=ALU.mult, op1=ALU.add), reads, writes)

    XA = sb("XA", [128, 16384], F32)
    XB = sb("XB", [128, 16384], F32)
    RC = sb("RC", [128, 8, 1024], BF16)
    RD = sb("RD", [128, 8, 1024], BF16)
    RF = sb("RF", [128, 4, 1024], F32)
    STG = sb("STG", [128, 4, 1024], F32)
    HB = sb("HB", [128, 2, 1024], BF16)
    ident = sb("identb", [128, 128], BF16)
    permb = sb("permb", [128, 128], BF16)
    identf = sb("identf", [128, 128], F32)
    ones = sb("ones", [128, 128], F32)
    gatew = sb("gatew_s", [128, 16, 128], BF16)
    vecs = sb("vecs_s", [128, NVEC2], F32)
    siluT = sb("siluT", [128, 8], BF16)
    cvs = sb("cvs", [128, 8], F32)
    stat = sb("stat", [128, 64], F32)
    rows = sb("rows", [1, 2, 512], F32)
    sinks = sb("sinks", [128, 4], F32)
    FS = sb("FS", [128, 32], F32)

    def carve(region, off_b, nbytes, dt, pattern=None, **kw):
        ap = region[:, off_b // 4:(off_b + nbytes) // 4]
        if dt != F32:
            ap = ap.bitcast(dt)
        if pattern:
            ap = ap.rearrange(pattern, **kw)
        return ap

    wlx = carve(XA, 0, 16384, BF16, "p (k n) -> p k n", k=8)
    wqk = carve(XA, 16384, 14336, BF16, "p (k n) -> p k n", k=8)
    wout = carve(XA, 16384, 16384, BF16, "p (k n) -> p k n", k=8)
    qT = carve(XA, 40960, 8192, BF16, "p (c n) -> p c n", c=4)
    ropeC = carve(XA, 49152, 4096, F32)
    ropeS = carve(XA, 53248, 4096, F32)
    masks = carve(XA, 57344, 4096, BF16, "p (m n) -> p m n", m=16)
    sinkrow = carve(XA, 61440, 2048, F32, "p (h n) -> p h n", h=4)
    actT = XA[:, :].bitcast(BF16).rearrange("p (c n) -> p c n", c=32)
    LXB = [carve(XB, 0 + i * 4160, 4144, F32, "p (s n) -> p s n", s=4) for i in range(2)]
    XC = carve(XB, 8320, 4096, F32)
    XCB = carve(XB, 12416, 2048, BF16)
    T1 = carve(XB, 14464, 4096, F32)
    T2 = carve(XB, 18560, 4096, F32)
    T3 = carve(XB, 22656, 4096, F32)
    HS = [carve(XB, 26752 + i * 4096, 4096, F32) for i in range(2)]
    GLG = carve(XB, 34944, 8192, BF16, "p (c n) -> p c n", c=4)
    kpad = [carve(XB, 43136 + i * 2560, 2560, BF16) for i in range(2)]
    vaug = [carve(XB, 48256 + i * 2560, 2560, BF16, "p (b n) -> p b n", b=10) for i in range(2)]
    PT = [carve(XB, 53376 + i * 1024, 1024, BF16) for i in range(6)]
    rrow = carve(XB, 59520, 2048, F32)
    Rsb = carve(XB, 61568, 2048, F32)
    wdn = XB[:, :].bitcast(BF16).rearrange("p (c n) -> p c n", c=32)
    MODW = [carve(XB, i * 8192, 8192, BF16, "p (k n) -> p k n", k=8) for i in range(4)]
    MODW += [RC[:, 4 * i:4 * i + 4, :].rearrange("p a n -> p (a n)").rearrange("p (k n) -> p k n", k=8) for i in range(2)]
    MODW += [RD[:, 4:8, :].rearrange("p a n -> p (a n)").rearrange("p (k n) -> p k n", k=8)]
    mixT = RD
    gs1 = RD[:, 0:2, :].bitcast(F32).rearrange("p a n -> p (a n)")
    shift1 = RD[:, 2:4, :].bitcast(F32).rearrange("p a n -> p (a n)")
    WUP = [RD[:, 2 * i:2 * i + 2, :].rearrange("p a n -> p (a n)").rearrange("p (k n) -> p k n", k=8) for i in range(4)]
    gg1 = RF[:, 0, :]
    gs2 = RF[:, 1, :]
    shift2 = RF[:, 2, :]
    gg2 = RF[:, 3, :]

    PS = [psum("ps%d" % i) for i in range(4)]
    T_bank = [P.tile("bank%d" % i) for i in range(8)]
    for tb in T_bank:
        tb.excl = True

    def bank_ap(b):
        return PS[b // 2][:, (b % 2) * 512:(b % 2) * 512 + 512]

    st = {"bank": 0, "pair": 0, "stg": 0, "hb": 0, "pt": 0, "rows": 0, "stat": 0}

    def next_bank():
        b = st["bank"]
        st["bank"] = (b + 1) % 8
        return bank_ap(b), T_bank[b]

    def next_pair():
        p = st["pair"]
        st["pair"] = (p + 1) % 4
        st["bank"] = (2 * p + 2) % 8
        return PS[p], [T_bank[2 * p], T_bank[2 * p + 1]]

    T_stg = [P.tile("stg%d" % i) for i in range(4)]

    def next_stg():
        i = st["stg"]
        st["stg"] = (i + 1) % 4
        return STG[:, i, :], T_stg[i]

    def pipeline(ntile, stages, ascending=False):
        ns = len(stages)
        for step in range(ntile + ns - 1):
            for k in (range(ns) if ascending else range(ns - 1, -1, -1)):
                t = step - k
                if 0 <= t < ntile:
                    stages[k](t)

    T_hb = [P.tile("hb%d" % i) for i in range(2)]

    def next_hb():
        i = st["hb"]
        st["hb"] = (i + 1) % 2
        return HB[:, i, :], T_hb[i]

    T_pt = [P.tile("pt%d" % i) for i in range(6)]

    def next_pt():
        i = st["pt"]
        st["pt"] = (i + 1) % 6
        return PT[i], T_pt[i]

    T_rows = [P.tile("rows0")]

    def next_rows():
        if st.get("rows_alt") is not None:
            return st["rows_alt"]
        return rows[0:1, 0:2, :], T_rows[0]

    T = {n: P.tile(n) for n in [
        "ident", "identf", "ones", "gatew", "vecs", "siluT", "cvs", "sinks", "FS", "FST",
        "wlx", "wqk", "wout", "ropeC", "ropeS", "masks", "sinkrow", "XC", "XCB", "T1", "T2", "T3", "HS0", "HS1",
        "kctx", "vctx", "rrow0", "rrow1", "Rsb0", "Rsb1", "gs1", "shift1", "gg1", "gs2", "shift2", "gg2", "wdn", "GLG"]}
    T_hT = [P.tile("hT%d" % t) for t in range(8)]
    T_qT = [P.tile("qT%d" % h) for h in range(2)]
    T_kp = [P.tile("kp%d" % h) for h in range(2)]
    T_va = [P.tile("va%d" % t) for t in range(8)]
    T_mixa = [P.tile("mixa%d_%d" % (i, g)) for i in range(8) for g in range(2)]
    T_mixl = [P.tile("mixl%d" % c) for c in range(4)]
    T_lxb = [P.tile("lxb%d" % i) for i in range(2)]
    T_act = [P.tile("actT%d" % c) for c in range(32)]
    T_wup = [P.tile("wup%d" % i) for i in range(4)]
    T_modw = [P.tile("modw%d" % i) for i in range(7)]
    T_x1s = [P.tile("x1s%d" % i) for i in range(8)]

    P.dma("sp", vecs[:, 0:NVEC], vecs_d, writes=[T["vecs"]])
    P.dma("sp", cvs[:], cvec_d, writes=[T["cvs"]])
    P.dma("pool", ident[:], ident_d, writes=[T["ident"]])
    T["perm"] = P.tile("perm")
    P.dma("pool", permb[:], perm_d, writes=[T["perm"]])
    memset("dve", ones[:], 1.0, [T["ones"]])
    act(siluT[:], cvs[:], AF.Silu, [T["cvs"]], [T["siluT"]])

    modw_src = modw_d.rearrange("(k p) n -> p k n", p=128)
    modw_buf = {0: 0, 1: 1, 2: 4, 3: 5, 4: 2, 5: 3, 6: 0, 7: 1, 8: 6, 9: 2, 10: 3, 11: 0}

    T_gate = P.tile("gate")

    def load_modw(n):
        buf = modw_buf[n]
        w = [T_modw[buf]] + ([T_gate] if n == 3 else [])
        r = [T_gate] if n == 4 else []
        P.dma("pool", MODW[buf], modw_src[:, :, n * 512:(n + 1) * 512], reads=r, writes=w, key=T_modw[buf])

    win_src = win_d.rearrange("(k p) n -> p k n", p=128)

    def gemv_piece(n):
        buf = modw_buf[n]
        bap, bt = next_bank()
        for k in range(8):
            mm(bap[0:1, :], siluT[:, k:k + 1], MODW[buf][:, k, :], k == 0, k == 7, [T["siluT"], T_modw[buf]], [bt], inc=(k == 7))
        return bap, bt

    def mod_part(n, kind, gidx, dst, dst_tile, half):
        bap, bt = gemv_piece(n)
        rw, rwt = next_rows()
        P.dma("sp", rw[0:1, 0, :], modb_d[0:1, n * 512:(n + 1) * 512], writes=[rwt])
        if kind != "shift":
            P.dma("sp", rw[0:1, 1, :], gvec_d[gidx:gidx + 1, half * 512:(half + 1) * 512], writes=[rwt])
        tt("dve", rw[0:1, 0, :], bap[0:1, :], rw[0:1, 0, :], ALU.add, [bt, rwt], [rwt])
        if kind == "scale":
            stt(rw[0:1, 0, :], rw[0:1, 0, :], 1.0, rw[0:1, 1, :], ALU.add, ALU.mult, [rwt], [rwt])
        elif kind == "gate":
            tt("dve", rw[0:1, 0, :], rw[0:1, 0, :], rw[0:1, 1, :], ALU.mult, [rwt], [rwt])
        b2, b2t = next_bank()
        mm(b2, ones[0:1, :], rw[0:1, 0, :], True, True, [T["ones"], rwt], [b2t])
        act(dst[:, half * 512:(half + 1) * 512], b2, AF.Copy, [b2t], [dst_tile])

    def mod_begin(n, kind, gidx, dst, dst_tile, half):
        bap, bt = gemv_piece(n)
        rw, rwt = next_rows()
        P.dma("sp", rw[0:1, 0, :], modb_d[0:1, n * 512:(n + 1) * 512], writes=[rwt])
        if kind != "shift":
            P.dma("sp", rw[0:1, 1, :], gvec_d[gidx:gidx + 1, half * 512:(half + 1) * 512], writes=[rwt])
        tt("dve", rw[0:1, 0, :], bap[0:1, :], rw[0:1, 0, :], ALU.add, [bt, rwt], [rwt])
        if kind == "scale":
            stt(rw[0:1, 0, :], rw[0:1, 0, :], 1.0, rw[0:1, 1, :], ALU.add, ALU.mult, [rwt], [rwt])
        elif kind == "gate":
            tt("dve", rw[0:1, 0, :], rw[0:1, 0, :], rw[0:1, 1, :], ALU.mult, [rwt], [rwt])
        return dict(rw=rw, rwt=rwt, dst=dst, dst_tile=dst_tile, half=half)

    def mod_mid(m):
        b2, b2t = next_bank()
        mm(b2, ones[0:1, :], m["rw"][0:1, 0, :], True, True, [T["ones"], m["rwt"]], [b2t])
        m["b2"], m["b2t"] = b2, b2t

    def mod_end(m):
        half = m["half"]
        act(m["dst"][:, half * 512:(half + 1) * 512], m["b2"], AF.Copy, [m["b2t"]], [m["dst_tile"]])

    mod_plan = [
        (0, "shift", 0, shift1, "shift1", 0), (1, "shift", 0, shift1, "shift1", 1),
        (2, "scale", 0, gs1, "gs1", 0), (3, "scale", 0, gs1, "gs1", 1),
        (4, "gate", 1, gg1, "gg1", 0), (5, "gate", 1, gg1, "gg1", 1),
        (6, "shift", 2, shift2, "shift2", 0), (7, "shift", 2, shift2, "shift2", 1),
        (8, "scale", 2, gs2, "gs2", 0), (9, "scale", 2, gs2, "gs2", 1),
        (10, "gate", 3, gg2, "gg2", 0), (11, "gate", 3, gg2, "gg2", 1),
    ]
    for n in range(4):
        load_modw(n)
    load_modw(4)
    load_modw(5)
    P.dma("sp", identf[:], ident_d, writes=[T["identf"]])
    P.dma("sp", sinks[:], sink_d, reads=[T_gate], writes=[T["sinks"]], key=T["sinks"])
    P.dma("sp", ropeC, ropec_d, reads=[T_gate], writes=[T["ropeC"]], key=T["ropeC"])
    P.dma("sp", ropeS, ropes_d, reads=[T_gate], writes=[T["ropeS"]], key=T["ropeS"])
    rows2 = STG[0:1, 3, :].rearrange("p (a n) -> p a n", a=2)
    pm = [None] * 4
    for n in range(4):
        pl = mod_plan[n]
        if n % 2 == 1:
            st["rows_alt"] = (rows2, T_stg[3])
        pm[n] = mod_begin(pl[0], pl[1], pl[2], pl[3], T[pl[4]], pl[5])
        st["rows_alt"] = None
        if n >= 1:
            mod_mid(pm[n - 1])
        if n >= 2:
            mod_end(pm[n - 2])
        if n == 1:
            load_modw(6)
            load_modw(7)
            load_modw(8)
            P.dma("pool", wqk, win_src[:, :, 1024:1920], writes=[T["wqk"]])
            P.dma("pool", wlx, win_src[:, :, 0:1024], writes=[T["wlx"]])
    mod_mid(pm[3])
    mod_end(pm[2])
    mod_end(pm[3])
    P.dma("pool", masks, mask_d.rearrange("p (m n) -> p m n", m=16), writes=[T["masks"]])
    P.dma("pool", gatew[:], gatew_d.rearrange("p (m n) -> p m n", m=16), writes=[T["gatew"]])

    act(vecs[:, V_C1:V_C1 + 8], vecs[:, V_LAM:V_LAM + 8], AF.Exp, [T["vecs"]], [T["vecs"]], scale=-1.0)
    act(vecs[:, V_C1:V_C1 + 8], vecs[:, V_C1:V_C1 + 8], AF.Ln, [T["vecs"]], [T["vecs"]], bias=1.0)
    ts("dve", vecs[:, V_HC1:V_HC1 + 8], vecs[:, V_C1:V_C1 + 8], -4.0, ALU.mult, [T["vecs"]], [T["vecs"]])
    ts("dve", vecs[:, V_C1:V_C1 + 8], vecs[:, V_C1:V_C1 + 8], -8.0, ALU.mult, [T["vecs"]], [T["vecs"]])
    ts("dve", vecs[:, V_HBR:V_HBR + 16], vecs[:, V_BR:V_BR + 16], 0.5, ALU.mult, [T["vecs"]], [T["vecs"]])
    act(sinks[:], sinks[:], AF.Exp, [T["sinks"]], [T["sinks"]])
    cp("dve", sinkrow, sinks[:].unsqueeze(2).broadcast_to([128, 4, 128]), [T["sinks"]], [T["sinkrow"]])

    for g in range(2):
        memset("dve", kpad[g], 0.0, [T_kp[0], T_kp[1], T["kctx"]])
        memset("dve", vaug[g], 0.0, T_va + [T["vctx"]])
    memset("dve", vaug[0][:, :, 64:65], 1.0, T_va + [T["vctx"]])
    memset("dve", vaug[1][:, :, 0:1], 1.0, T_va + [T["vctx"]])
    cv_src = cv_d.rearrange("(b p) n -> p b n", p=128)
    P.dma("pool", vaug[0][:, 8:10, 0:64], cv_src[:, :, 0:64], writes=[T["vctx"]])
    P.dma("pool", vaug[1][:, 8:10, 64:128], cv_src[:, :, 64:128], writes=[T["vctx"]])
    ckst, ckt = next_stg()
    P.dma("sp", ckst[:, 0:256].rearrange("p (b n) -> p b n", b=2), ck_d.rearrange("(b p) n -> p b n", p=128), writes=[ckt])
    for j in range(2):
        bap, bt = next_bank()
        trp(bap[:, 0:128], ckst[:, j * 128:(j + 1) * 128], identf[:], [ckt, T["identf"]], [bt])
        act(kpad[0][0:64, 1024 + j * 128:1152 + j * 128], bap[0:64, 0:128], AF.Copy, [bt], [T["kctx"]])
        act(kpad[1][64:128, 1024 + j * 128:1152 + j * 128], bap[64:128, 0:128], AF.Copy, [bt], [T["kctx"]])

    chk(0)

    def rstd_from(src_ap, src_tiles, junk_ap, junk_tile):
        c = st["stat"]
        st["stat"] += 2
        tl = P.tile("stat%d" % c)
        act(junk_ap, src_ap, AF.Square, src_tiles, [tl, junk_tile], accum_out=stat[:, c:c + 1])
        act(stat[:, c + 1:c + 2], stat[:, c:c + 1], AF.Ln, [tl], [tl], scale=1.0 / D, bias=1e-6)
        act(stat[:, c + 1:c + 2], stat[:, c + 1:c + 2], AF.Exp, [tl], [tl], scale=-0.5)
        return stat[:, c + 1:c + 2], tl

    def transpose_to(hb_ap, hb_tile, dstT, dst_tile, t):
        bap, bt = next_bank()
        bb = bap.bitcast(BF16)
        for j in range(8):
            trp(bb[:, j * 128:(j + 1) * 128], hb_ap[:, j * 128:(j + 1) * 128], ident[:], [hb_tile, T["ident"]], [bt], inc=(j == 7))
        act(dstT[:, :, t * 128:(t + 1) * 128], bb.rearrange("p (k n) -> p k n", k=8), AF.Copy, [bt], [dst_tile])

    chk(1)
    hT = RC
    for t in range(NTT):
        P.alias(T_hT[t], [T_modw[4], T_modw[5]])
    A_st = {}

    A_x = {}

    def A_load(t):
        xt, xtt = next_stg()
        P.dma("sp", xt, x_d[t * 128:(t + 1) * 128, :], writes=[xtt])
        A_x[t] = (xt, xtt)

    A_load(0)
    A_load(1)

    def A_s0(t):
        if t + 2 < NTT:
            A_load(t + 2)
        xt, xtt = A_x[t]
        hb, hbt = next_hb()
        r, rt_ = rstd_from(xt, [xtt], hb, hbt)
        act(xt, xt, AF.Identity, [xtt, rt_], [xtt], scale=r)
        tt("dve", xt, xt, gs1, ALU.mult, [xtt, T["gs1"]], [xtt])
        tt("dve", hb, xt, shift1, ALU.add, [xtt, T["shift1"]], [hbt])
        A_st[t] = (hb, hbt)

    A_tr = {}
    A_mod = {"begun": None, "mid": None}

    def A_s1(t):
        hb, hbt = A_st[t]
        bap, bt = next_bank()
        bb = bap.bitcast(BF16)
        for j in range(8):
            trp(bb[:, j * 128:(j + 1) * 128], hb[:, j * 128:(j + 1) * 128], ident[:], [hbt, T["ident"]], [bt], inc=(j == 7))
        A_tr[t] = (bb, bt)
        if A_mod["begun"] is not None:
            mod_mid(A_mod["begun"])
            A_mod["mid"] = A_mod["begun"]
            A_mod["begun"] = None
        if t < 5:
            pl = mod_plan[4 + t]
            A_mod["begun"] = mod_begin(pl[0], pl[1], pl[2], pl[3], T[pl[4]], pl[5])
        if t < 3:
            load_modw(9 + t)

    def A_s2(t):
        bb, bt = A_tr[t]
        act(hT[:, :, t * 128:(t + 1) * 128], bb.rearrange("p (k n) -> p k n", k=8), AF.Copy, [bt], [T_hT[t]])
        if A_mod["mid"] is not None:
            mod_end(A_mod["mid"])
            A_mod["mid"] = None

    pipeline(NTT, [A_s0, A_s1, A_s2], ascending=True)
    if A_mod["begun"] is not None:
        mod_mid(A_mod["begun"])
        mod_end(A_mod["begun"])
        A_mod["begun"] = None
    if A_mod["mid"] is not None:
        mod_end(A_mod["mid"])
        A_mod["mid"] = None

    chk(2)
    def proj_fm(wt, wtile, col, th):
        bap, bt = next_bank()
        for k in range(8):
            mm(bap, wt[:, k, col:col + 128], hT[:, k, th * 512:(th + 1) * 512], k == 0, k == 7,
               [wtile] + T_hT[th * 4:th * 4 + 4], [bt], inc=(k == 7))
        return bap, bt

    QK0 = 1024
    for t in range(NTT):
        bap, bt = next_bank()
        for k in range(8):
            mm(bap[:, 0:256], hT[:, k, t * 128:(t + 1) * 128], wqk[:, k, C_KV - QK0:C_KV - QK0 + 256], k == 0, k == 7,
               [T["wqk"], T_hT[t]], [bt], inc=(k == 7))
        kv, kvt = next_stg()
        act(kv[:, 0:256], bap[:, 0:256], AF.Copy, [bt], [kvt])
        P.dma("sp", nk_d[t * 128:(t + 1) * 128, :], kv[:, 0:128], reads=[kvt], is_output=True)
        P.dma("sp", nv_d[t * 128:(t + 1) * 128, :], kv[:, 128:256], reads=[kvt], is_output=True)
        cp("dve", vaug[0][:, t, 0:64], bap[:, 128:192], [bt], [T_va[t]])
        cp("dve", vaug[1][:, t, 64:128], bap[:, 192:256], [bt], [T_va[t]])
    chk(21)

    rope_items = []
    for th in range(2):
        rope_items.append((C_K, th, [(kpad[0][0:64, th * 512:(th + 1) * 512], 0, 64, [T_kp[th]]),
                                     (kpad[1][64:128, th * 512:(th + 1) * 512], 64, 128, [T_kp[th]])]))
        for c in range(4):
            rope_items.append((C_Q + c * 128, th, [(qT[:, c, th * 512:(th + 1) * 512], 0, 128, [T_qT[th]])]))
    R_st = {}

    def R_s0(n):
        col, th, outs = rope_items[n]
        a, at = proj_fm(wqk, T["wqk"], col - QK0, th)
        R_st[n] = dict(a=a, at=at)

    def R_s1(n):
        d = R_st[n]
        qb, qbt = next_hb()
        act(qb[:, 0:512], d["a"], AF.Copy, [d["at"]], [qbt])
        d.update(qb=qb, qbt=qbt)

    def R_s2(n):
        col, th, outs = rope_items[n]
        d = R_st[n]
        b, btl = next_bank()
        mm(b, permb[:], d["qb"][:, 0:512], True, True, [T["perm"], d["qbt"]], [btl])
        t1, t1t = next_stg()
        tt("dve", t1[:, 0:512], d["a"], ropeC[:, th * 512:(th + 1) * 512], ALU.mult, [d["at"], T["ropeC"]], [t1t])
        tt("dve", t1[:, 512:1024], b, ropeS[:, th * 512:(th + 1) * 512], ALU.mult, [btl, T["ropeS"]], [t1t])
        for (oap, lo, hi, tiles) in outs:
            tt("dve", oap, t1[lo:hi, 0:512], t1[lo:hi, 512:1024], ALU.add, [t1t], tiles)

    pipeline(len(rope_items), [R_s0, R_s1, R_s2])
    P.alias(T["GLG"], T_modw[0:4])
    for c in range(4):
        for th in range(2):
            a, at = proj_fm(wlx, T["wlx"], C_LG[c], th)
            act(GLG[:, c, th * 512:(th + 1) * 512], a, AF.Gelu, [at], [T["GLG"]])
    chk(3)

    for n in range(9, 12):
        pl = mod_plan[n]
        mod_part(pl[0], pl[1], pl[2], pl[3], T[pl[4]], pl[5])
    P.alias(T["wout"], [T["wqk"]])
    P.dma("pool", wout, wout_d.rearrange("(k p) n -> p k n", p=128), writes=[T["wout"]])
    for nm in ["XC", "XCB", "T1", "T2", "T3", "HS0", "HS1"]:
        P.alias(T[nm], T_modw[0:4])
    for tl in T_lxb:
        P.alias(tl, T_modw[0:4])
    for tl in T_mixa + T_mixl:
        P.alias(tl, [T["gs1"], T["shift1"], T_modw[6]])

    chk(4)
    S_BANKS = [0, 1]
    P_BANK = 2
    X_BANK = [3, 4]
    R_BANK = 5
    L_BANKS = [6, 7]
    lst = {"s": 0, "l": 0}

    def next_sbank():
        b = S_BANKS[lst["s"] % 2]
        lst["s"] += 1
        return bank_ap(b), T_bank[b]

    def next_lbank():
        b = L_BANKS[lst["l"] % 2]
        lst["l"] += 1
        return bank_ap(b), T_bank[b]

    units = []
    for i in range(8):
        for g in range(2):
            blocks = []
            if i > 0:
                blocks.append(("loc", i - 1, 0))
            blocks.append(("loc", i, None))
            if i < 7:
                blocks.append(("loc", i + 1, 1))
            blocks.append(("ctx", 0, None))
            blocks.append(("ctx", 1, None))
            for bi, (kind, j, slot) in enumerate(blocks):
                units.append(dict(i=i, g=g, kind=kind, j=j, slot=slot, first=(bi == 0), last=(bi == len(blocks) - 1)))

    def unit_S(u):
        i, g = u["i"], u["g"]
        sap, stl = next_sbank()
        if u["kind"] == "loc":
            kcol = u["j"] * 128
            ktile = T_kp[u["j"] // 4]
        else:
            kcol = 1024 + u["j"] * 128
            ktile = T["kctx"]
        if u["slot"] is None:
            mm(sap.rearrange("p (c n) -> p c n", c=4), kpad[g][:, kcol:kcol + 128], qT[:, :, i * 128:(i + 1) * 128], True, True,
               [ktile, T_qT[i // 4]], [stl])
        else:
            mm(sap.rearrange("p (c n) -> p c n", c=4), kpad[g][:, kcol:kcol + 128], qT[:, :, i * 128:(i + 1) * 128], True, False,
               [ktile, T_qT[i // 4]], [stl], inc=False)
            m = masks[:, i * 2 + u["slot"], :]
            mm(sap.rearrange("p (c n) -> p c n", c=4), ident[:], m.unsqueeze(1).broadcast_to([128, 4, 128]), False, True,
               [T["ident"], T["masks"]], [stl])
        u["sap"], u["stl"] = sap, stl

    def unit_exp(u):
        i = u["i"]
        pt, ptt = next_pt()
        if u["kind"] == "ctx":
            act(pt, u["sap"], AF.Exp, [u["stl"], T["vecs"]], [ptt], scale=0.125, bias=vecs[:, V_CTXB:V_CTXB + 1])
        else:
            act(pt, u["sap"], AF.Exp, [u["stl"]], [ptt], scale=0.125)
        u["pt"], u["ptt"] = pt, ptt

    def unit_PV(u):
        i, g = u["i"], u["g"]
        xap, xt = bank_ap(X_BANK[g]), T_bank[X_BANK[g]]
        if u["kind"] == "loc":
            vblk, vtile = u["j"], T_va[u["j"]]
        else:
            vblk, vtile = 8 + u["j"], T["vctx"]
        mm(xap, vaug[g][:, vblk, :], u["pt"], u["first"], u["last"], [vtile, u["ptt"]], [xt], inc=u["last"])
        if u["last"]:
            if g == 0:
                tt("dve", rrow[64:65, :], xap[64:65, :], sinkrow[64:65].rearrange("p h n -> p (h n)"), ALU.add, [xt, T["sinkrow"]], [T["rrow0"]])
            else:
                tt("dve", rrow[0:1, :], xap[0:1, :], sinkrow[0:1].rearrange("p h n -> p (h n)"), ALU.add, [xt, T["sinkrow"]], [T["rrow1"]])
            pending.append([1, 1, i, g])

    def norm_stage(stage, i, g):
        xap, xt = bank_ap(X_BANK[g]), T_bank[X_BANK[g]]
        rap, rt = bank_ap(R_BANK), T_bank[R_BANK]
        p = 64 if g == 0 else 0
        lo, hi = (0, 64) if g == 0 else (64, 128)
        rr_t = T["rrow0"] if g == 0 else T["rrow1"]
        rs_t = T["Rsb0"] if g == 0 else T["Rsb1"]
        if stage == 1:
            for c in range(4):
                mm(rap[:, c:c + 1], rrow[p:p + 1, c * 128:(c + 1) * 128], ones[p:p + 1, 0:1], True, True, [rr_t, T["ones"]], [rt], inc=(c == 3))
        elif stage == 2:
            recip(rT[:, :], rap[:, 0:4], [rt], [T_rT])
            cp("dve", bc[:, :, :], rT[:, :].unsqueeze(2).broadcast_to([128, 4, 64]), [T_rT], [T_bc])
        elif stage == 3:
            for c in range(4):
                mm(rap[lo:hi, c * 128:(c + 1) * 128], bc[:, c, :], identf[:, :], True, True, [T_bc, T["identf"]], [rt], inc=(c == 3))
        else:
            act(Rsb[lo:hi, :], rap[lo:hi, :], AF.Copy, [rt], [rs_t])
            tt("dve", mixT[lo:hi, 0:4, i * 128:(i + 1) * 128], xap[lo:hi, :].rearrange("p (c n) -> p c n", c=4),
               Rsb[lo:hi, :].rearrange("p (c n) -> p c n", c=4), ALU.mult, [xt, rs_t], [T_mixa[i * 2 + g]])

    pending = []
    rT = sb("rT", [128, 4], F32)
    bc = sb("bcn", [128, 4, 64], F32)
    T_rT = P.tile("rT")
    T_bc = P.tile("bc")

    def run_pending():
        for p in list(pending):
            p[0] -= 1
            if p[0] <= 0:
                pending.remove(p)
                norm_stage(p[1], p[2], p[3])
                if p[1] < 4:
                    pending.append([1, p[1] + 1, p[2], p[3]])

    def attention_steps():
        n = len(units)
        unit_S(units[0])
        yield
        for k in range(n):
            unit_exp(units[k])
            if k + 1 < n:
                unit_S(units[k + 1])
            run_pending()
            if k >= 1:
                unit_PV(units[k - 1])
            yield
        unit_PV(units[n - 1])
        yield
        for _ in range(6):
            run_pending()
            yield

    T1s = [T1, STG[:, 0, :]]
    T2s = [T2, STG[:, 1, :]]
    T3s = [T3, STG[:, 2, :]]
    XCs = [XC, STG[:, 3, :]]
    XCBs = [XCB, HB[:, 0, :]]
    TT1 = [T["T1"], P.tile("T1b")]
    TT2 = [T["T2"], P.tile("T2b")]
    TT3 = [T["T3"], P.tile("T3b")]
    TXC = [T["XC"], P.tile("XCb")]
    TXCB = [T["XCB"], P.tile("XCBb")]
    for tl in [TT1[1], TT2[1], TT3[1], TXC[1]]:
        P.alias(tl, T_stg)
    P.alias(TXCB[1], T_hb)
    fl = vecs[:, V_FLAG:V_FLAG + 1]

    def lru_prep(c):
        bi = c % 2
        lxb = LXB[bi]
        lxt = T_lxb[bi]
        xc = XCs[bi]
        xct = TXC[bi]

        def lx_mm(th):
            bap, bt = bank_ap(P_BANK), T_bank[P_BANK]
            for k in range(8):
                mm(bap, wlx[:, k, C_LX[c]:C_LX[c] + 128], hT[:, k, th * 512:(th + 1) * 512], k == 0, k == 7,
                   [T["wlx"]] + T_hT[th * 4:th * 4 + 4], [bt], inc=(k == 7))
            return bap, bt

        def lx_cp(b, th):
            act(lxb[:, 2 * th:2 * th + 2, 2:258], b[0].rearrange("p (s n) -> p s n", s=2), AF.Copy, [b[1]], [lxt])

        b0 = lx_mm(0)
        yield
        lx_cp(b0, 0)
        yield
        b1 = lx_mm(1)
        yield
        lx_cp(b1, 1)
        memset("dve", lxb[:, 0:1, 0:2], 0.0, [lxt])
        memset("dve", lxb[:, 3:4, 258:259], 0.0, [lxt])
        ts("dve", lxb[:, 1:4, 0:2], lxb[:, 0:3, 256:258], fl, ALU.mult, [lxt, T["vecs"]], [lxt])
        ts("dve", lxb[:, 0:3, 258:259], lxb[:, 1:4, 2:3], fl, ALU.mult, [lxt, T["vecs"]], [lxt])
        xc3 = xc.rearrange("p (s n) -> p s n", s=4)
        act(xc3, lxb[:, :, 2:258], AF.Identity, [lxt, T["vecs"]], [xct],
            scale=vecs[:, V_CONVW + 2 * 4 + c:V_CONVW + 2 * 4 + c + 1], bias=vecs[:, V_CONVB + c:V_CONVB + c + 1])
        yield
        for j in (0, 1, 3):
            stt(xc3, lxb[:, :, j:j + 256], vecs[:, V_CONVW + j * 4 + c:V_CONVW + j * 4 + c + 1], xc3, ALU.mult, ALU.add,
                [lxt, T["vecs"], xct], [xct])
            yield
        cp("dve", XCBs[bi], xc, [xct], [TXCB[bi]])
        yield

    def lru_rec(c):
        bi = c % 2
        xc, xct = XCs[bi], TXC[bi]
        xcb, xcbt = XCBs[bi], TXCB[bi]
        hold = {}

        def g_mm(d, gi, th):
            b = L_BANKS[d]
            bap, bt = bank_ap(b), T_bank[b]
            mm(bap, gatew[:, (gi * 2 + d) * 4 + c, :], xcb[:, th * 512:(th + 1) * 512], True, True, [T["gatew"], xcbt], [bt])
            hold[d] = (bap, bt)

        def g_tanh(d, dst, dst_t, th, bias_col):
            bap, bt = hold[d]
            act(dst[:, th * 512:(th + 1) * 512], bap, AF.Tanh, [bt, T["vecs"]], [dst_t], scale=0.5, bias=vecs[:, bias_col:bias_col + 1])

        for gi, dsts, dts, bcol in ((0, T1s, TT1, V_HBR), (1, T3s, TT3, V_HBI)):
            for th in range(2):
                for d in range(2):
                    g_mm(d, gi, th)
                yield
                for d in range(2):
                    g_tanh(d, dsts[d], dts[d], th, bcol + d * 4 + c)
                if gi == 1 and th == 1:
                    for d in range(2):
                        vb = d * 4 + c
                        act(T2s[d], T1s[d], AF.Exp, [TT1[d], T["vecs"]], [TT2[d]], scale=vecs[:, V_HC1 + vb:V_HC1 + vb + 1], bias=vecs[:, V_HC1 + vb:V_HC1 + vb + 1])
                yield
        for d in range(2):
            tt("dve", T1s[d], T2s[d], T2s[d], ALU.mult, [TT2[d]], [TT1[d]])
        yield
        for d in range(2):
            act(T1s[d], T1s[d], AF.Sqrt, [TT1[d]], [TT1[d]], scale=-0.25, bias=0.25)
        for d in range(2):
            tt("dve", T1s[d], T1s[d], xc, ALU.mult, [TT1[d], xct], [TT1[d]])
        yield
        for d in range(2):
            stt(T3s[d], T3s[d], 1.0, T1s[d], ALU.add, ALU.mult, [TT3[d], TT1[d]], [TT3[d]])
            a3 = T2s[d].rearrange("p (s n) -> p s n", s=4)
            if d == 0:
                ts("dve", a3[:, 1:4, 0:1], a3[:, 1:4, 0:1], fl, ALU.mult, [TT2[d], T["vecs"]], [TT2[d]])
            else:
                ts("dve", a3[:, 0:3, 255:256], a3[:, 0:3, 255:256], fl, ALU.mult, [TT2[d], T["vecs"]], [TT2[d]])
        yield "tail"
        for d in range(2):
            vb = d * 4 + c
            hs = HS[d]
            hst = T["HS%d" % d]
            h0 = vecs[:, V_H0 + vb:V_H0 + vb + 1]
            fs3 = FS[:, :].rearrange("p (s q) -> p s q", s=4)[:, :, vb:vb + 1]
            if d == 0:
                scan(hs, T2s[d], T3s[d], h0, [TT2[d], TT3[d], T["vecs"]], [hst])
                cp("dve", fs3, hs.rearrange("p (s n) -> p s n", s=4)[:, :, 255:256], [hst], [T["FS"]])
            else:
                scan(hs[:, ::-1], T2s[d][:, ::-1], T3s[d][:, ::-1], h0, [TT2[d], TT3[d], T["vecs"]], [hst])
                cp("dve", fs3, hs.rearrange("p (s n) -> p s n", s=4)[:, :, 0:1], [hst], [T["FS"]])
            yield
        tt("dve", HS[0], HS[0], HS[1], ALU.add, [T["HS0"], T["HS1"]], [T["HS0"]])
        tt("dve", mixT[:, 4 + c, :], HS[0], GLG[:, c, :], ALU.mult, [T["HS0"], T["GLG"]], [T_mixl[c]])
        yield

    def merged(g1, g2):
        d1 = d2 = False
        while not (d1 and d2):
            if not d1:
                try:
                    next(g1)
                    yield
                except StopIteration:
                    d1 = True
            if not d2:
                try:
                    next(g2)
                    yield
                except StopIteration:
                    d2 = True

    def empty():
        return
        yield

    def merged_n(gens):
        live = list(gens)
        while live:
            for g in list(live):
                try:
                    next(g)
                    yield
                except StopIteration:
                    live.remove(g)

    def all_lru():
        for _ in lru_prep(0):
            yield
        recs = {c: lru_rec(c) for c in range(4)}

        def head(c):
            for v in recs[c]:
                if v == "tail":
                    return
                yield

        def tail(c):
            for v in recs[c]:
                yield

        for c in range(4):
            parts = [head(c)]
            if c > 0:
                parts.append(tail(c - 1))
            if c < 3:
                parts.append(lru_prep(c + 1))
            for _ in merged_n(parts):
                yield
        for _ in tail(3):
            yield

    ga = attention_steps()
    gl = all_lru()
    a_done = l_done = False
    RATIO = 1
    while not (a_done and l_done):
        for _ in range(RATIO):
            if not a_done:
                try:
                    next(ga)
                except StopIteration:
                    a_done = True
        if not l_done:
            try:
                next(gl)
            except StopIteration:
                l_done = True

    for tl in T_stg:
        P.alias(tl, TT1 + TT2 + TT3 + TXC)
    for tl in T_hb:
        P.alias(tl, TXCB)

    chk(5)
    bap, bt = next_bank()
    trp(bap[0:32, 0:128], FS[:, :], identf[:], [T["FS"], T["identf"]], [bt])
    FST = HB[:, 1, :].bitcast(F32)[0:32, 0:128]
    act(FST, bap[0:32, 0:128], AF.Copy, [bt], [T_hb[1]])
    P.dma("sp", nst_d, FST, reads=[T_hb[1]], is_output=True)

    xb_tiles = [T[n] for n in ["XC", "XCB", "T1", "T2", "T3", "HS0", "HS1", "kctx", "vctx", "rrow0", "rrow1", "Rsb0", "Rsb1", "GLG"]] \
        + T_lxb + T_kp + T_va + T_pt + T_modw[0:4]
    P.alias(T["wdn"], xb_tiles)
    wdn_src = wdn_d.rearrange("(c p) n -> p c n", p=128)

    chk(6)
    h2T = RC
    T_h2T = [P.tile("h2T%d" % t) for t in range(8)]
    for t in range(8):
        P.alias(T_h2T[t], T_hT)
    EB = [carve(XA, off, 4096, F32) for off in (0, 4096, 8192, 12288, 40960, 45056, 49152, 53248)]
    T_eb = [P.tile("eb%d" % i) for i in range(8)]
    for tl in T_eb:
        P.alias(tl, [T["wlx"], T["ropeC"], T["ropeS"]] + T_qT)
    EJ = carve(XA, 57344, 2048, BF16)
    T_ej = P.tile("ej")
    P.alias(T_ej, [T["masks"]])
    est = {"x": 0, "t": 0, "u": 0, "p": 0, "tb": 0}

    def e_xbuf():
        i = est["x"]
        est["x"] = (i + 1) % 4
        return EB[i], T_eb[i]

    def e_tbuf():
        i = 4 + est["t"]
        est["t"] ^= 1
        return EB[i], T_eb[i]

    def e_ubuf():
        i = 6 + est["u"]
        est["u"] ^= 1
        return EB[i], T_eb[i]

    def e_pair():
        p = est["p"]
        est["p"] = (p + 1) % 3
        return PS[p], [T_bank[2 * p], T_bank[2 * p + 1]]

    def e_tbank():
        b = 6 + est["tb"]
        est["tb"] ^= 1
        return bank_ap(b), T_bank[b]

    E_st = {}
    wup_src = wup_d.rearrange("(k p) n -> p k n", p=128)
    NPIECE = 16

    def load_wup(pc):
        P.dma("pool", WUP[pc % 4], wup_src[:, :, pc * 256:(pc + 1) * 256], writes=[T_wup[pc % 4]])

    def E_s0(t):
        pp, ppt = e_pair()
        for hf in range(2):
            for c in range(8):
                mm(pp[:, hf * 512:(hf + 1) * 512], mixT[:, c, t * 128:(t + 1) * 128], wout[:, c, hf * 512:(hf + 1) * 512], c == 0, c == 7,
                   [T["wout"], T_mixa[2 * t], T_mixa[2 * t + 1]] + T_mixl, [ppt[hf]], inc=(c == 7))
        xt, xtt = e_xbuf()
        P.dma("sp", xt, x_d[t * 128:(t + 1) * 128, :], writes=[xtt])
        E_st[t] = dict(pp=pp, ppt=ppt, xt=xt, xtt=xtt)
        if t == NTT - 1:
            for i in range(4):
                P.alias(T_wup[i], T_mixa + T_mixl)
            for pc in range(3):
                load_wup(pc)

    def E_s1(t):
        d = E_st[t]
        r, rt_ = rstd_from(d["pp"][:, :], d["ppt"], EJ, T_ej)
        d.update(r=r, rt=rt_)

    def E_s2(t):
        d = E_st[t]
        pp, ppt, xt, xtt = d["pp"], d["ppt"], d["xt"], d["xtt"]
        tmp, tmpt = e_tbuf()
        stt(tmp, pp[:, :], d["r"], gg1, ALU.mult, ALU.mult, ppt + [d["rt"], T["gg1"]], [tmpt])
        tt("dve", xt, xt, tmp, ALU.add, [xtt, tmpt], [xtt])
        P.dma("sp", x1s_d[t * 128:(t + 1) * 128, :], xt, reads=[xtt], writes=[T_x1s[t]], key=xtt)

    def E_s3(t):
        d = E_st[t]
        xt, xtt = d["xt"], d["xtt"]
        r2, rt2 = rstd_from(xt, [xtt], EJ, T_ej)
        tmp2, tmp2t = e_ubuf()
        act(tmp2, xt, AF.Identity, [xtt, rt2], [tmp2t], scale=r2)
        d.update(tmp2=tmp2, tmp2t=tmp2t)

    def E_s4(t):
        d = E_st[t]
        tmp2, tmp2t = d["tmp2"], d["tmp2t"]
        hb, hbt = next_hb()
        tt("dve", tmp2, tmp2, gs2, ALU.mult, [tmp2t, T["gs2"]], [tmp2t])
        tt("dve", hb, tmp2, shift2, ALU.add, [tmp2t, T["shift2"]], [hbt])
        d.update(hb=hb, hbt=hbt)

    def E_s5(t):
        d = E_st[t]
        bap, bt = e_tbank()
        bb = bap.bitcast(BF16)
        for j in range(8):
            trp(bb[:, j * 128:(j + 1) * 128], d["hb"][:, j * 128:(j + 1) * 128], ident[:], [d["hbt"], T["ident"]], [bt], inc=(j == 7))
        d.update(bb=bb, bt=bt)

    def E_s6(t):
        d = E_st[t]
        act(h2T[:, :, t * 128:(t + 1) * 128], d["bb"].rearrange("p (k n) -> p k n", k=8), AF.Copy, [d["bt"]], [T_h2T[t]])

    pipeline(NTT, [E_s0, E_s1, E_s2, E_s3, E_s4, E_s5, E_s6])
    st["bank"] = 0
    st["pair"] = 0

    chk(7)
    for c in range(32):
        P.alias(T_act[c], [T["wlx"], T["wout"], T["wqk"], T["ropeC"], T["ropeS"], T["masks"], T["sinkrow"]] + T_qT + T_eb + [T_ej])
    for pc in range(NPIECE):
        if pc + 3 < NPIECE:
            load_wup(pc + 3)
        if pc % 2 == 1:
            q = pc // 2
            P.dma("pool", wdn[:, q * 4:(q + 1) * 4, :], wdn_src[:, q * 4:(q + 1) * 4, :], writes=[T["wdn"]])
        for cc in range(2):
            c = pc * 2 + cc
            for th in range(2):
                bap, bt = next_bank()
                for k in range(8):
                    mm(bap, WUP[pc % 4][:, k, cc * 128:(cc + 1) * 128], h2T[:, k, th * 512:(th + 1) * 512], k == 0, k == 7,
                       [T_wup[pc % 4]] + T_h2T[th * 4:th * 4 + 4], [bt], inc=(k == 7))
                rl, rlt = next_stg()
                act(rl[:, 0:512], bap, AF.Relu, [bt], [rlt])
                tt("dve", actT[:, c, th * 512:(th + 1) * 512], rl[:, 0:512], rl[:, 0:512], ALU.mult, [rlt], [T_act[c]])

    chk(8)
    G_st = {}

    def G_s0(t):
        pp, ppt = next_pair()
        for hf in range(2):
            for c in range(32):
                mm(pp[:, hf * 512:(hf + 1) * 512], actT[:, c, t * 128:(t + 1) * 128], wdn[:, c, hf * 512:(hf + 1) * 512], c == 0, c == 31,
                   [T["wdn"], T_act[c]], [ppt[hf]], inc=(c == 31))
        xt, xtt = next_stg()
        P.dma("sp", xt, x1s_d[t * 128:(t + 1) * 128, :], reads=[T_x1s[t]], writes=[xtt])
        G_st[t] = (pp, ppt, xt, xtt)

    def G_s1(t):
        pp, ppt, xt, xtt = G_st[t]
        tmp, tmpt = next_stg()
        r, rt_ = rstd_from(pp[:, :], ppt, tmp.bitcast(BF16)[:, 0:1024], tmpt)
        stt(tmp, pp[:, :], r, gg2, ALU.mult, ALU.mult, ppt + [rt_, T["gg2"]], [tmpt])
        tt("dve", xt, xt, tmp, ALU.add, [xtt, tmpt], [xtt])
        P.dma("sp", y_d[t * 128:(t + 1) * 128, :], xt, reads=[xtt], is_output=True)

    pipeline(NTT, [G_s0, G_s1])


N_Q_HEADS = 8
HEAD_DIM = 64


def _host_layout(inputs):
    f32 = np.float32
    w_in = np.asarray(inputs["w_in"][0], f32)
    q_cols = []
    for c in range(4):
        q_cols += list(range(c * 64, c * 64 + 64)) + list(range((4 + c) * 64, (4 + c) * 64 + 64))
    q_cols = np.array(q_cols)

    def partner(cols):
        base = (cols // 32) * 32
        return base + ((cols % 32) + 16) % 32

    k_cols = np.arange(512, 640)
    kv_cols = np.arange(512, 768)
    lx_cols = np.arange(768, 1280)
    lg_cols = np.arange(1280, 1792)
    ext = []
    for c in range(4):
        ext += list(lx_cols[c * 128:(c + 1) * 128]) + list(lg_cols[c * 128:(c + 1) * 128])
    ext += list(kv_cols) + list(q_cols) + list(k_cols)
    ext = np.array(ext)
    assert ext.shape[0] == NEXT
    w_in_ext = np.ascontiguousarray(w_in[:, ext])
    w_out = np.asarray(inputs["w_out"][0], f32)
    rows = list(q_cols) + list(range(512, 1024))
    w_out_p = np.ascontiguousarray(w_out[rows, :])

    gvec = np.stack([inputs["g_pre_mix"][0], inputs["g_post_mix"][0], inputs["g_pre_mlp"][0], inputs["g_post_mlp"][0]]).astype(f32)

    def pl(v):
        return np.asarray(v, f32).reshape(4, 128).T

    conv_w = inputs["conv_w"][0]
    base = np.zeros((128, NVEC), f32)
    for j in range(4):
        base[:, V_CONVW + j * 4:V_CONVW + j * 4 + 4] = pl(conv_w[j])
    base[:, V_CONVB:V_CONVB + 4] = pl(inputs["conv_b"][0])
    for d in range(2):
        base[:, V_BR + d * 4:V_BR + d * 4 + 4] = pl(inputs["lru_b_r"][0, d])
        base[:, V_BI + d * 4:V_BI + d * 4 + 4] = pl(inputs["lru_b_i"][0, d])
        base[:, V_LAM + d * 4:V_LAM + d * 4 + 4] = pl(inputs["lru_lam"][0, d])

    gatew = np.zeros((128, 16, 128), f32)
    for gi, nm in enumerate(["lru_w_r", "lru_w_i"]):
        w = np.asarray(inputs[nm][0], f32)
        for d in range(2):
            for c in range(4):
                for b in range(2):
                    gatew[b * 64:(b + 1) * 64, (gi * 2 + d) * 4 + c, b * 64:(b + 1) * 64] = w[d, c * 2 + b]
    gatew = gatew.reshape(128, 16 * 128)

    sink = np.asarray(inputs["attn_sink"][0], f32)
    sinkmat = np.zeros((128, 4), f32)
    sinkmat[64, :] = sink[0:4]
    sinkmat[0, :] = sink[4:8]
    sel = np.zeros((128, 128), f32)
    sel[64, 0:64] = 1.0
    sel[0, 64:128] = 1.0
    ident = np.eye(128, dtype=f32)
    ff = np.arange(128)
    perm = np.zeros((128, 128), f32)
    perm[partner(ff), ff] = 1.0

    f = np.arange(128)
    dd = f % 64
    half = dd // 32
    jj = dd % 32
    fi = jj % 16
    first = jj < 16
    inv = (10000.0 ** (-np.arange(16, dtype=np.float32) / 16)).astype(f32)
    tt = np.arange(NT)
    rowp = (tt // 64).astype(f32)
    colp = (tt % 64).astype(f32)
    pos = np.where(half[:, None] == 0, rowp[None, :], colp[None, :]).astype(f32)
    ang = (pos * inv[fi][:, None]).astype(f32)
    rc_s = np.cos(ang).astype(f32)
    rs_s = (np.sin(ang) * np.where(first, -1.0, 1.0)[:, None]).astype(f32)
    rc_p = np.ones((128, NT), f32)
    rs_p = np.zeros((128, NT), f32)

    b = np.arange(128)[:, None]
    a = np.arange(128)[None, :]
    m_s = np.zeros((128, 16, 128), f32)
    m_p = np.zeros((128, 16, 128), f32)
    for i in range(8):
        m_s[:, i * 2 + 0, :] = (b >= a)
        m_s[:, i * 2 + 1, :] = (b <= a)
        m_p[:, i * 2 + 0, :] = 1.0 if (i % 2 == 1) else 0.0
        m_p[:, i * 2 + 1, :] = 1.0 if (i % 2 == 0) else 0.0
    m_s = ((1.0 - m_s) * -240000.0).astype(f32).reshape(128, 2048)
    m_p = ((1.0 - m_p) * -240000.0).astype(f32).reshape(128, 2048)

    shared = {
        "mod_w": np.ascontiguousarray(np.asarray(inputs["mod_w"][0], f32)),
        "mod_b": np.asarray(inputs["mod_b"][0], f32).reshape(1, 6 * D),
        "gvec": gvec, "w_in_ext": w_in_ext, "w_out_p": w_out_p,
        "w_up": np.ascontiguousarray(np.asarray(inputs["mlp_w_up"][0], f32)),
        "w_down": np.ascontiguousarray(np.asarray(inputs["mlp_w_down"][0], f32)),
        "sinkmat": sinkmat, "ident": ident, "gatew": gatew, "perm": perm,
    }
    in_maps = []
    xp = np.asarray(inputs["x_prompt"], f32)
    xs = np.asarray(inputs["x_sample"], f32)
    for core in range(8):
        m = dict(shared)
        vec = base.copy()
        if core < 4:
            m["x"] = np.ascontiguousarray(xp[core * 4:(core + 1) * 4].reshape(NT, D))
            cv = np.asarray(inputs["c_ctx"], f32)
            m["ck"] = np.ascontiguousarray(np.asarray(inputs["cache_k"][0, 0], f32).reshape(256, 128))
            m["cv"] = np.ascontiguousarray(np.asarray(inputs["cache_v"][0, 0], f32).reshape(256, 128))
            vec[:, V_FLAG] = 0.0
            vec[:, V_CTXB] = -30000.0
            m["rope_c"], m["rope_s"], m["masks"] = rc_p, rs_p, m_p
        else:
            bi = core - 4
            m["x"] = np.ascontiguousarray(xs[bi])
            cv = np.asarray(inputs["c"][bi], f32)
            m["ck"] = np.ascontiguousarray(np.asarray(inputs["cache_k"][bi, 0], f32).reshape(256, 128))
            m["cv"] = np.ascontiguousarray(np.asarray(inputs["cache_v"][bi, 0], f32).reshape(256, 128))
            vec[:, V_FLAG] = 1.0
            vec[:, V_CTXB] = 0.0
            for d in range(2):
                vec[:, V_H0 + d * 4:V_H0 + d * 4 + 4] = pl(inputs["state_lru"][bi, 0, d])
            m["rope_c"], m["rope_s"], m["masks"] = rc_s, rs_s, m_s
        m["cvecT"] = np.ascontiguousarray(cv.reshape(8, 128).T)
        m["vecs"] = vec
        in_maps.append(m)
    return in_maps


_CACHE = {}


def kernel(**inputs):
    in_maps = _host_layout(inputs)
    if "nc" not in _CACHE:
        _CACHE["nc"] = build()
    nc = _CACHE["nc"]
    res = run_bass_kernel_spmd(nc, in_maps, core_ids=list(range(8)))
    r = res.results
    y_prompt = np.stack([r[c]["y"] for c in range(4)]).reshape(16, 256, D).astype(np.float32)
    y_sample = np.stack([r[4 + c]["y"] for c in range(4)]).reshape(4, 1024, D).astype(np.float32)
    new_k = np.stack([r[c]["new_k"] for c in range(4)]).reshape(16, 1, 256, 2, 64).astype(np.float32)
    new_v = np.stack([r[c]["new_v"] for c in range(4)]).reshape(16, 1, 256, 2, 64).astype(np.float32)
    ns = np.stack([r[c]["new_state_t"].reshape(4, 2, 4 * 128) for c in range(4)]).reshape(16, 1, 2, 512).astype(np.float32)
    return (y_prompt, y_sample, new_k, new_v, ns)
```

```python
import numpy as np
import concourse.bass as bass
import concourse.mybir as mybir
from concourse.bass_utils import run_bass_kernel_spmd

F32 = mybir.dt.float32
BF16 = mybir.dt.bfloat16
AF = mybir.ActivationFunctionType
ALU = mybir.AluOpType

ENGS = ("pe", "act", "dve", "pool", "sp")


class Tile:
    __slots__ = ("name", "w", "r", "dsem", "dcnt", "excl")

    def __init__(self, name):
        self.name = name
        self.excl = False
        self.w = None
        self.r = []
        self.dsem = None
        self.dcnt = 0


class Prog:
    def __init__(self, nc, strict_same=True):
        self.nc = nc
        self.ops = {e: [] for e in ENGS}
        self.cnt = {e: 0 for e in ENGS}
        self.sem = {}
        self.seen = {e: {} for e in ENGS}
        self.strict_same = strict_same
        self.stack = []
        self.out_tokens = []
        for e in ("pe", "act", "dve", "pool"):
            self.sem[e] = self._newsem("e_" + e)

    def _newsem(self, name):
        cm = self.nc.semaphore(name)
        s = cm.__enter__()
        self.stack.append(cm)
        return s

    def tile(self, name):
        return Tile(name)

    def alias(self, new, olds):
        for o in olds:
            if o.w is not None:
                new.r.append(o.w)
            new.r.extend(o.r)

    def _waits(self, eng, reads, writes):
        toks = []
        for t in reads:
            if t.w is not None:
                toks.append(t.w)
        for t in writes:
            if t.w is not None:
                toks.append(t.w)
            toks.extend(t.r)
        need = {}
        for (s, v, src) in toks:
            if src == eng and (eng in ("pe", "sp") or not self.strict_same):
                continue
            k = id(s)
            if self.seen[eng].get(k, 0) >= v:
                continue
            if k not in need or need[k][1] < v:
                need[k] = (s, v)
        for k, (s, v) in need.items():
            self.seen[eng][k] = v
            self.ops[eng].append(lambda e, s=s, v=v: e.wait_ge(s, v))

    def op(self, eng, fn, reads=(), writes=(), inc=True):
        ex = [t for t in reads if t.excl]
        if ex:
            reads = [t for t in reads if not t.excl]
            writes = list(writes) + ex
        self._waits(eng, reads, writes)
        s = self.sem[eng]
        tok = (s, self.cnt[eng] + 1, eng)
        if inc:
            self.cnt[eng] += 1
            self.ops[eng].append(lambda e, fn=fn, s=s: fn(e).then_inc(s, 1))
        else:
            self.ops[eng].append(lambda e, fn=fn: fn(e))
        for t in reads:
            t.r.append(tok)
        for t in writes:
            t.w = tok
            t.r = []
        return tok

    def dma(self, eng, out, in_, reads=(), writes=(), key=None, is_output=False, **kw):
        self._waits(eng, reads, writes)
        kt = key if key is not None else (writes[0] if writes else reads[0])
        if kt.dsem is None:
            kt.dsem = self._newsem("d_" + kt.name)
        kt.dcnt += 16
        s = kt.dsem
        tok = (s, kt.dcnt, "dma")
        self.ops[eng].append(
            lambda e, out=out, in_=in_, s=s, kw=kw: e.dma_start(out=out, in_=in_, **kw).then_inc(s, 16))
        for t in reads:
            t.r.append(tok)
        for t in writes:
            t.w = tok
            t.r = []
        if is_output:
            self.out_tokens.append(tok)
        return tok

    def finish(self):
        need = {}
        for (s, v, src) in self.out_tokens:
            k = id(s)
            if k not in need or need[k][1] < v:
                need[k] = (s, v)
        for k, (s, v) in need.items():
            self.ops["sp"].append(lambda e, s=s, v=v: e.wait_ge(s, v))

    def emit(self):
        nc = self.nc
        ops = self.ops
        with nc.Block() as block:
            @block.tensor
            def _(e):
                for f in ops["pe"]:
                    f(e)

            @block.scalar
            def _(e):
                for f in ops["act"]:
                    f(e)

            @block.vector
            def _(e):
                for f in ops["dve"]:
                    f(e)

            @block.gpsimd
            def _(e):
                for f in ops["pool"]:
                    f(e)

            @block.sync
            def _(e):
                for f in ops["sp"]:
                    f(e)


D = 1024
NT = 1024
NTT = 8
DFF = 4096
NEXT = 1920
C_LX = [0, 256, 512, 768]
C_LG = [128, 384, 640, 896]
C_KV = 1024
C_Q = 1280
C_K = 1792
NVEC = 64
V_CONVW = 0
V_CONVB = 16
V_BR = 20
V_BI = 28
V_LAM = 36
V_H0 = 44
V_FLAG = 52
V_CTXB = 53
V_C1 = 54
V_HC1 = 64
V_HBR = 72
V_HBI = 80
NVEC2 = 96


class _Stop(Exception):
    pass


def build(debug=False, stop=None):
    nc = bass.Bass("TRN2", target_bir_lowering=False)
    P = Prog(nc)
    try:
        _build_body(nc, P, debug, stop)
    except _Stop:
        pass
    P.finish()
    P.emit()
    return nc


def _build_body(nc, P, debug, stop):
    def chk(n):
        if stop == n:
            raise _Stop()

    def din(name, shape, dt=F32):
        return nc.dram_tensor(name, list(shape), dt, kind="ExternalInput").ap()

    def dout(name, shape, dt=F32):
        return nc.dram_tensor(name, list(shape), dt, kind="ExternalOutput").ap()

    x_d = din("x", [NT, D])
    cvec_d = din("cvecT", [128, 8])
    modw_d = din("mod_w", [D, 6 * D])
    modb_d = din("mod_b", [1, 6 * D])
    gvec_d = din("gvec", [4, D])
    win_d = din("w_in_ext", [D, NEXT])
    wout_d = din("w_out_p", [D, D])
    wup_d = din("w_up", [D, DFF])
    wdn_d = din("w_down", [DFF, D])
    ck_d = din("ck", [256, 128])
    cv_d = din("cv", [256, 128])
    vecs_d = din("vecs", [128, NVEC])
    sink_d = din("sinkmat", [128, 4])
    ident_d = din("ident", [128, 128])
    perm_d = din("perm", [128, 128])
    ropec_d = din("rope_c", [128, NT])
    ropes_d = din("rope_s", [128, NT])
    mask_d = din("masks", [128, 16 * 128])
    gatew_d = din("gatew", [128, 16 * 128])

    y_d = dout("y", [NT, D])
    nk_d = dout("new_k", [NT, 128])
    nv_d = dout("new_v", [NT, 128])
    nst_d = dout("new_state_t", [32, 128])
    x1s_d = nc.dram_tensor("x1_scratch", [NT, D], F32).ap()

    cms = []

    def sb(name, shape, dt):
        cm = nc.sbuf_tensor(name, list(shape), dt)
        cms.append(cm)
        return cm.__enter__()

    def psum(name):
        cm = nc.psum_tensor(name, [128, 1024], F32)
        cms.append(cm)
        return cm.__enter__()

    def tt(eng, out, in0, in1, op, reads, writes):
        return P.op(eng, lambda e: e.tensor_tensor(out=out, in0=in0, in1=in1, op=op), reads, writes)

    def ts(eng, out, in0, s1, op0, reads, writes, s2=None, op1=None):
        if op1 is None:
            return P.op(eng, lambda e: e.tensor_scalar(out=out, in0=in0, scalar1=s1, scalar2=None, op0=op0), reads, writes)
        return P.op(eng, lambda e: e.tensor_scalar(out=out, in0=in0, scalar1=s1, scalar2=s2, op0=op0, op1=op1), reads, writes)

    def stt(out, in0, scalar, in1, op0, op1, reads, writes):
        return P.op("dve", lambda e: e.scalar_tensor_tensor(out=out, in0=in0, scalar=scalar, in1=in1, op0=op0, op1=op1), reads, writes)

    def act(out, in_, func, reads, writes, scale=1.0, bias=0.0, accum_out=None):
        if accum_out is None:
            return P.op("act", lambda e: e.activation(out=out, in_=in_, func=func, scale=scale, bias=bias), reads, writes)
        return P.op("act", lambda e: e.activation(out=out, in_=in_, func=func, scale=scale, bias=bias, accum_out=accum_out), reads, writes)

    def mm(out, lhsT, rhs, start, stop_, reads, writes, inc=True):
        return P.op("pe", lambda e: e.matmul(out, lhsT=lhsT, rhs=rhs, start=start, stop=stop_), reads, writes, inc=inc)

    def trp(out, in_, idn, reads, writes, inc=True):
        return P.op("pe", lambda e: e.transpose(out=out, in_=in_, identity=idn), reads, writes, inc=inc)

    def cp(eng, out, in_, reads, writes):
        return P.op(eng, lambda e: e.tensor_copy(out=out, in_=in_), reads, writes)

    def memset(eng, ap, val, writes):
        return P.op(eng, lambda e: e.memset(ap, val), (), writes)

    def recip(out, in_, reads, writes):
        return P.op("dve", lambda e: e.reciprocal(out=out, in_=in_), reads, writes)

    def scan(out, d0, d1, init, reads, writes):
        return P.op("dve", lambda e: e.tensor_tensor_scan(out=out, data0=d0, data1=d1, initial=init, op0=ALU.mult, op1=ALU.add), reads, writes)

    XA = sb("XA", [128, 16384], F32)
    XB = sb("XB", [128, 16384], F32)
    RC = sb("RC", [128, 8, 1024], BF16)
    RD = sb("RD", [128, 8, 1024], BF16)
    RF = sb("RF", [128, 4, 1024], F32)
    STG = sb("STG", [128, 4, 1024], F32)
    HB = sb("HB", [128, 2, 1024], BF16)
    ident = sb("identb", [128, 128], BF16)
    permb = sb("permb", [128, 128], BF16)
    identf = sb("identf", [128, 128], F32)
    ones = sb("ones", [128, 128], F32)
    gatew = sb("gatew_s", [128, 16, 128], BF16)
    vecs = sb("vecs_s", [128, NVEC2], F32)
    siluT = sb("siluT", [128, 8], BF16)
    cvs = sb("cvs", [128, 8], F32)
    stat = sb("stat", [128, 64], F32)
    rows = sb("rows", [1, 2, 512], F32)
    sinks = sb("sinks", [128, 4], F32)
    FS = sb("FS", [128, 32], F32)

    def carve(region, off_b, nbytes, dt, pattern=None, **kw):
        ap = region[:, off_b // 4:(off_b + nbytes) // 4]
        if dt != F32:
            ap = ap.bitcast(dt)
        if pattern:
            ap = ap.rearrange(pattern, **kw)
        return ap

    wlx = carve(XA, 0, 16384, BF16, "p (k n) -> p k n", k=8)
    wqk = carve(XA, 16384, 14336, BF16, "p (k n) -> p k n", k=8)
    wout = carve(XA, 16384, 16384, BF16, "p (k n) -> p k n", k=8)
    qT = carve(XA, 40960, 8192, BF16, "p (c n) -> p c n", c=4)
    ropeC = carve(XA, 49152, 4096, F32)
    ropeS = carve(XA, 53248, 4096, F32)
    masks = carve(XA, 57344, 4096, BF16, "p (m n) -> p m n", m=16)
    sinkrow = carve(XA, 61440, 2048, F32, "p (h n) -> p h n", h=4)
    actT = XA[:, :].bitcast(BF16).rearrange("p (c n) -> p c n", c=32)
    LXB = [carve(XB, 0 + i * 4160, 4144, F32, "p (s n) -> p s n", s=4) for i in range(2)]
    XC = carve(XB, 8320, 4096, F32)
    XCB = carve(XB, 12416, 2048, BF16)
    T1 = carve(XB, 14464, 4096, F32)
    T2 = carve(XB, 18560, 4096, F32)
    T3 = carve(XB, 22656, 4096, F32)
    HS = [carve(XB, 26752 + i * 4096, 4096, F32) for i in range(2)]
    GLG = carve(XB, 34944, 8192, BF16, "p (c n) -> p c n", c=4)
    kpad = [carve(XB, 43136 + i * 2560, 2560, BF16) for i in range(2)]
    vaug = [carve(XB, 48256 + i * 2560, 2560, BF16, "p (b n) -> p b n", b=10) for i in range(2)]
    PT = [carve(XB, 53376 + i * 1024, 1024, BF16) for i in range(6)]
    rrow = carve(XB, 59520, 2048, F32)
    Rsb = carve(XB, 61568, 2048, F32)
    wdn = XB[:, :].bitcast(BF16).rearrange("p (c n) -> p c n", c=32)
    MODW = [carve(XB, i * 8192, 8192, BF16, "p (k n) -> p k n", k=8) for i in range(4)]
    MODW += [RC[:, 4 * i:4 * i + 4, :].rearrange("p a n -> p (a n)").rearrange("p (k n) -> p k n", k=8) for i in range(2)]
    MODW += [RD[:, 4:8, :].rearrange("p a n -> p (a n)").rearrange("p (k n) -> p k n", k=8)]
    mixT = RD
    gs1 = RD[:, 0:2, :].bitcast(F32).rearrange("p a n -> p (a n)")
    shift1 = RD[:, 2:4, :].bitcast(F32).rearrange("p a n -> p (a n)")
    WUP = [RD[:, 2 * i:2 * i + 2, :].rearrange("p a n -> p (a n)").rearrange("p (k n) -> p k n", k=8) for i in range(4)]
    gg1 = RF[:, 0, :]
    gs2 = RF[:, 1, :]
    shift2 = RF[:, 2, :]
    gg2 = RF[:, 3, :]

    PS = [psum("ps%d" % i) for i in range(4)]
    T_bank = [P.tile("bank%d" % i) for i in range(8)]
    for tb in T_bank:
        tb.excl = True

    def bank_ap(b):
        return PS[b // 2][:, (b % 2) * 512:(b % 2) * 512 + 512]

    st = {"bank": 0, "pair": 0, "stg": 0, "hb": 0, "pt": 0, "rows": 0, "stat": 0}

    def next_bank():
        b = st["bank"]
        st["bank"] = (b + 1) % 8
        return bank_ap(b), T_bank[b]

    def next_pair():
        p = st["pair"]
        st["pair"] = (p + 1) % 4
        st["bank"] = (2 * p + 2) % 8
        return PS[p], [T_bank[2 * p], T_bank[2 * p + 1]]

    T_stg = [P.tile("stg%d" % i) for i in range(4)]

    def next_stg():
        i = st["stg"]
        st["stg"] = (i + 1) % 4
        return STG[:, i, :], T_stg[i]

    def pipeline(ntile, stages, ascending=False):
        ns = len(stages)
        for step in range(ntile + ns - 1):
            for k in (range(ns) if ascending else range(ns - 1, -1, -1)):
                t = step - k
                if 0 <= t < ntile:
                    stages[k](t)

    T_hb = [P.tile("hb%d" % i) for i in range(2)]

    def next_hb():
        i = st["hb"]
        st["hb"] = (i + 1) % 2
        return HB[:, i, :], T_hb[i]

    T_pt = [P.tile("pt%d" % i) for i in range(6)]

    def next_pt():
        i = st["pt"]
        st["pt"] = (i + 1) % 6
        return PT[i], T_pt[i]

    T_rows = [P.tile("rows0")]

    def next_rows():
        if st.get("rows_alt") is not None:
            return st["rows_alt"]
        return rows[0:1, 0:2, :], T_rows[0]

    T = {n: P.tile(n) for n in [
        "ident", "identf", "ones", "gatew", "vecs", "siluT", "cvs", "sinks", "FS", "FST",
        "wlx", "wqk", "wout", "ropeC", "ropeS", "masks", "sinkrow", "XC", "XCB", "T1", "T2", "T3", "HS0", "HS1",
        "kctx", "vctx", "rrow0", "rrow1", "Rsb0", "Rsb1", "gs1", "shift1", "gg1", "gs2", "shift2", "gg2", "wdn", "GLG"]}
    T_hT = [P.tile("hT%d" % t) for t in range(8)]
    T_qT = [P.tile("qT%d" % h) for h in range(2)]
    T_kp = [P.tile("kp%d" % h) for h in range(2)]
    T_va = [P.tile("va%d" % t) for t in range(8)]
    T_mixa = [P.tile("mixa%d_%d" % (i, g)) for i in range(8) for g in range(2)]
    T_mixl = [P.tile("mixl%d" % c) for c in range(4)]
    T_lxb = [P.tile("lxb%d" % i) for i in range(2)]
    T_act = [P.tile("actT%d" % c) for c in range(32)]
    T_wup = [P.tile("wup%d" % i) for i in range(4)]
    T_modw = [P.tile("modw%d" % i) for i in range(7)]
    T_x1s = [P.tile("x1s%d" % i) for i in range(8)]

    P.dma("sp", vecs[:, 0:NVEC], vecs_d, writes=[T["vecs"]])
    P.dma("sp", cvs[:], cvec_d, writes=[T["cvs"]])
    P.dma("pool", ident[:], ident_d, writes=[T["ident"]])
    T["perm"] = P.tile("perm")
    P.dma("pool", permb[:], perm_d, writes=[T["perm"]])
    memset("dve", ones[:], 1.0, [T["ones"]])
    act(siluT[:], cvs[:], AF.Silu, [T["cvs"]], [T["siluT"]])

    modw_src = modw_d.rearrange("(k p) n -> p k n", p=128)
    modw_buf = {0: 0, 1: 1, 2: 4, 3: 5, 4: 2, 5: 3, 6: 0, 7: 1, 8: 6, 9: 2, 10: 3, 11: 0}

    T_gate = P.tile("gate")

    def load_modw(n):
        buf = modw_buf[n]
        w = [T_modw[buf]] + ([T_gate] if n == 3 else [])
        r = [T_gate] if n == 4 else []
        P.dma("pool", MODW[buf], modw_src[:, :, n * 512:(n + 1) * 512], reads=r, writes=w, key=T_modw[buf])

    win_src = win_d.rearrange("(k p) n -> p k n", p=128)

    def gemv_piece(n):
        buf = modw_buf[n]
        bap, bt = next_bank()
        for k in range(8):
            mm(bap[0:1, :], siluT[:, k:k + 1], MODW[buf][:, k, :], k == 0, k == 7, [T["siluT"], T_modw[buf]], [bt], inc=(k == 7))
        return bap, bt

    def mod_part(n, kind, gidx, dst, dst_tile, half):
        bap, bt = gemv_piece(n)
        rw, rwt = next_rows()
        P.dma("sp", rw[0:1, 0, :], modb_d[0:1, n * 512:(n + 1) * 512], writes=[rwt])
        if kind != "shift":
            P.dma("sp", rw[0:1, 1, :], gvec_d[gidx:gidx + 1, half * 512:(half + 1) * 512], writes=[rwt])
        tt("dve", rw[0:1, 0, :], bap[0:1, :], rw[0:1, 0, :], ALU.add, [bt, rwt], [rwt])
        if kind == "scale":
            stt(rw[0:1, 0, :], rw[0:1, 0, :], 1.0, rw[0:1, 1, :], ALU.add, ALU.mult, [rwt], [rwt])
        elif kind == "gate":
            tt("dve", rw[0:1, 0, :], rw[0:1, 0, :], rw[0:1, 1, :], ALU.mult, [rwt], [rwt])
        b2, b2t = next_bank()
        mm(b2, ones[0:1, :], rw[0:1, 0, :], True, True, [T["ones"], rwt], [b2t])
        act(dst[:, half * 512:(half + 1) * 512], b2, AF.Copy, [b2t], [dst_tile])

    def mod_begin(n, kind, gidx, dst, dst_tile, half):
        bap, bt = gemv_piece(n)
        rw, rwt = next_rows()
        P.dma("sp", rw[0:1, 0, :], modb_d[0:1, n * 512:(n + 1) * 512], writes=[rwt])
        if kind != "shift":
            P.dma("sp", rw[0:1, 1, :], gvec_d[gidx:gidx + 1, half * 512:(half + 1) * 512], writes=[rwt])
        tt("dve", rw[0:1, 0, :], bap[0:1, :], rw[0:1, 0, :], ALU.add, [bt, rwt], [rwt])
        if kind == "scale":
            stt(rw[0:1, 0, :], rw[0:1, 0, :], 1.0, rw[0:1, 1, :], ALU.add, ALU.mult, [rwt], [rwt])
        elif kind == "gate":
            tt("dve", rw[0:1, 0, :], rw[0:1, 0, :], rw[0:1, 1, :], ALU.mult, [rwt], [rwt])
        return dict(rw=rw, rwt=rwt, dst=dst, dst_tile=dst_tile, half=half)

    def mod_mid(m):
        b2, b2t = next_bank()
        mm(b2, ones[0:1, :], m["rw"][0:1, 0, :], True, True, [T["ones"], m["rwt"]], [b2t])
        m["b2"], m["b2t"] = b2, b2t

    def mod_end(m):
        half = m["half"]
        act(m["dst"][:, half * 512:(half + 1) * 512], m["b2"], AF.Copy, [m["b2t"]], [m["dst_tile"]])

    mod_plan = [
        (0, "shift", 0, shift1, "shift1", 0), (1, "shift", 0, shift1, "shift1", 1),
        (2, "scale", 0, gs1, "gs1", 0), (3, "scale", 0, gs1, "gs1", 1),
        (4, "gate", 1, gg1, "gg1", 0), (5, "gate", 1, gg1, "gg1", 1),
        (6, "shift", 2, shift2, "shift2", 0), (7, "shift", 2, shift2, "shift2", 1),
        (8, "scale", 2, gs2, "gs2", 0), (9, "scale", 2, gs2, "gs2", 1),
        (10, "gate", 3, gg2, "gg2", 0), (11, "gate", 3, gg2, "gg2", 1),
    ]
    for n in range(4):
        load_modw(n)
    load_modw(4)
    load_modw(5)
    P.dma("sp", identf[:], ident_d, writes=[T["identf"]])
    rows2 = STG[0:1, 3, :].rearrange("p (a n) -> p a n", a=2)
    pm = [None] * 4
    for n in range(4):
        pl = mod_plan[n]
        if n % 2 == 1:
            st["rows_alt"] = (rows2, T_stg[3])
        pm[n] = mod_begin(pl[0], pl[1], pl[2], pl[3], T[pl[4]], pl[5])
        st["rows_alt"] = None
        if n >= 1:
            mod_mid(pm[n - 1])
        if n >= 2:
            mod_end(pm[n - 2])
        if n == 1:
            load_modw(6)
            load_modw(7)
            load_modw(8)
            P.dma("pool", wqk, win_src[:, :, 1024:1920], writes=[T["wqk"]])
            P.dma("pool", wlx, win_src[:, :, 0:1024], writes=[T["wlx"]])
    mod_mid(pm[3])
    mod_end(pm[2])
    mod_end(pm[3])
    P.dma("pool", masks, mask_d.rearrange("p (m n) -> p m n", m=16), writes=[T["masks"]])
    P.dma("pool", gatew[:], gatew_d.rearrange("p (m n) -> p m n", m=16), writes=[T["gatew"]])

    act(vecs[:, V_C1:V_C1 + 8], vecs[:, V_LAM:V_LAM + 8], AF.Exp, [T["vecs"]], [T["vecs"]], scale=-1.0)
    act(vecs[:, V_C1:V_C1 + 8], vecs[:, V_C1:V_C1 + 8], AF.Ln, [T["vecs"]], [T["vecs"]], bias=1.0)
    ts("dve", vecs[:, V_HC1:V_HC1 + 8], vecs[:, V_C1:V_C1 + 8], -4.0, ALU.mult, [T["vecs"]], [T["vecs"]])
    ts("dve", vecs[:, V_C1:V_C1 + 8], vecs[:, V_C1:V_C1 + 8], -8.0, ALU.mult, [T["vecs"]], [T["vecs"]])
    ts("dve", vecs[:, V_HBR:V_HBR + 16], vecs[:, V_BR:V_BR + 16], 0.5, ALU.mult, [T["vecs"]], [T["vecs"]])

    for g in range(2):
        memset("dve", kpad[g], 0.0, [T_kp[0], T_kp[1], T["kctx"]])
        memset("dve", vaug[g], 0.0, T_va + [T["vctx"]])
    memset("dve", vaug[0][:, :, 64:65], 1.0, T_va + [T["vctx"]])
    memset("dve", vaug[1][:, :, 0:1], 1.0, T_va + [T["vctx"]])
    cv_src = cv_d.rearrange("(b p) n -> p b n", p=128)
    P.dma("pool", vaug[0][:, 8:10, 0:64], cv_src[:, :, 0:64], writes=[T["vctx"]])
    P.dma("pool", vaug[1][:, 8:10, 64:128], cv_src[:, :, 64:128], writes=[T["vctx"]])
    ckst, ckt = next_stg()
    P.dma("sp", ckst[:, 0:256].rearrange("p (b n) -> p b n", b=2), ck_d.rearrange("(b p) n -> p b n", p=128), writes=[ckt])
    for j in range(2):
        bap, bt = next_bank()
        trp(bap[:, 0:128], ckst[:, j * 128:(j + 1) * 128], identf[:], [ckt, T["identf"]], [bt])
        act(kpad[0][0:64, 1024 + j * 128:1152 + j * 128], bap[0:64, 0:128], AF.Copy, [bt], [T["kctx"]])
        act(kpad[1][64:128, 1024 + j * 128:1152 + j * 128], bap[64:128, 0:128], AF.Copy, [bt], [T["kctx"]])

    chk(0)

    def rstd_from(src_ap, src_tiles, junk_ap, junk_tile):
        c = st["stat"]
        st["stat"] += 2
        tl = P.tile("stat%d" % c)
        act(junk_ap, src_ap, AF.Square, src_tiles, [tl, junk_tile], accum_out=stat[:, c:c + 1])
        act(stat[:, c + 1:c + 2], stat[:, c:c + 1], AF.Ln, [tl], [tl], scale=1.0 / D, bias=1e-6)
        act(stat[:, c + 1:c + 2], stat[:, c + 1:c + 2], AF.Exp, [tl], [tl], scale=-0.5)
        return stat[:, c + 1:c + 2], tl

    def transpose_to(hb_ap, hb_tile, dstT, dst_tile, t):
        bap, bt = next_bank()
        bb = bap.bitcast(BF16)
        for j in range(8):
            trp(bb[:, j * 128:(j + 1) * 128], hb_ap[:, j * 128:(j + 1) * 128], ident[:], [hb_tile, T["ident"]], [bt], inc=(j == 7))
        act(dstT[:, :, t * 128:(t + 1) * 128], bb.rearrange("p (k n) -> p k n", k=8), AF.Copy, [bt], [dst_tile])

    chk(1)
    hT = RC
    for t in range(NTT):
        P.alias(T_hT[t], [T_modw[4], T_modw[5]])
    A_st = {}

    A_x = {}

    def A_load(t):
        xt, xtt = next_stg()
        P.dma("sp", xt, x_d[t * 128:(t + 1) * 128, :], writes=[xtt])
        A_x[t] = (xt, xtt)

    A_load(0)
    A_load(1)

    def A_s0(t):
        if t + 2 < NTT:
            A_load(t + 2)
        xt, xtt = A_x[t]
        hb, hbt = next_hb()
        r, rt_ = rstd_from(xt, [xtt], hb, hbt)
        act(xt, xt, AF.Identity, [xtt, rt_], [xtt], scale=r)
        tt("dve", xt, xt, gs1, ALU.mult, [xtt, T["gs1"]], [xtt])
        tt("dve", hb, xt, shift1, ALU.add, [xtt, T["shift1"]], [hbt])
        A_st[t] = (hb, hbt)

    A_tr = {}
    A_mod = {"begun": None, "mid": None}

    def A_s1(t):
        hb, hbt = A_st[t]
        bap, bt = next_bank()
        bb = bap.bitcast(BF16)
        for j in range(8):
            trp(bb[:, j * 128:(j + 1) * 128], hb[:, j * 128:(j + 1) * 128], ident[:], [hbt, T["ident"]], [bt], inc=(j == 7))
        A_tr[t] = (bb, bt)
        if A_mod["begun"] is not None:
            mod_mid(A_mod["begun"])
            A_mod["mid"] = A_mod["begun"]
            A_mod["begun"] = None
        if t < 5:
            pl = mod_plan[4 + t]
            A_mod["begun"] = mod_begin(pl[0], pl[1], pl[2], pl[3], T[pl[4]], pl[5])
        if t < 3:
            load_modw(9 + t)

    def A_s2(t):
        bb, bt = A_tr[t]
        act(hT[:, :, t * 128:(t + 1) * 128], bb.rearrange("p (k n) -> p k n", k=8), AF.Copy, [bt], [T_hT[t]])
        if A_mod["mid"] is not None:
            mod_end(A_mod["mid"])
            A_mod["mid"] = None

    pipeline(NTT, [A_s0, A_s1, A_s2], ascending=True)
    if A_mod["begun"] is not None:
        mod_mid(A_mod["begun"])
        mod_end(A_mod["begun"])
        A_mod["begun"] = None
    if A_mod["mid"] is not None:
        mod_end(A_mod["mid"])
        A_mod["mid"] = None

    P.dma("sp", sinks[:], sink_d, writes=[T["sinks"]])
    P.dma("sp", ropeC, ropec_d, writes=[T["ropeC"]])
    P.dma("sp", ropeS, ropes_d, writes=[T["ropeS"]])
    act(sinks[:], sinks[:], AF.Exp, [T["sinks"]], [T["sinks"]])
    cp("dve", sinkrow, sinks[:].unsqueeze(2).broadcast_to([128, 4, 128]), [T["sinks"]], [T["sinkrow"]])

    chk(2)
    def proj_fm(wt, wtile, col, th):
        bap, bt = next_bank()
        for k in range(8):
            mm(bap, wt[:, k, col:col + 128], hT[:, k, th * 512:(th + 1) * 512], k == 0, k == 7,
               [wtile] + T_hT[th * 4:th * 4 + 4], [bt], inc=(k == 7))
        return bap, bt

    QK0 = 1024
    for t in range(NTT):
        bap, bt = next_bank()
        for k in range(8):
            mm(bap[:, 0:256], hT[:, k, t * 128:(t + 1) * 128], wqk[:, k, C_KV - QK0:C_KV - QK0 + 256], k == 0, k == 7,
               [T["wqk"], T_hT[t]], [bt], inc=(k == 7))
        kv, kvt = next_stg()
        act(kv[:, 0:256], bap[:, 0:256], AF.Copy, [bt], [kvt])
        P.dma("sp", nk_d[t * 128:(t + 1) * 128, :], kv[:, 0:128], reads=[kvt], is_output=True)
        P.dma("sp", nv_d[t * 128:(t + 1) * 128, :], kv[:, 128:256], reads=[kvt], is_output=True)
        cp("dve", vaug[0][:, t, 0:64], bap[:, 128:192], [bt], [T_va[t]])
        cp("dve", vaug[1][:, t, 64:128], bap[:, 192:256], [bt], [T_va[t]])
    chk(21)

    rope_items = []
    for th in range(2):
        rope_items.append((C_K, th, [(kpad[0][0:64, th * 512:(th + 1) * 512], 0, 64, [T_kp[th]]),
                                     (kpad[1][64:128, th * 512:(th + 1) * 512], 64, 128, [T_kp[th]])]))
        for c in range(4):
            rope_items.append((C_Q + c * 128, th, [(qT[:, c, th * 512:(th + 1) * 512], 0, 128, [T_qT[th]])]))
    R_st = {}

    def R_s0(n):
        col, th, outs = rope_items[n]
        a, at = proj_fm(wqk, T["wqk"], col - QK0, th)
        R_st[n] = dict(a=a, at=at)

    def R_s1(n):
        d = R_st[n]
        qb, qbt = next_hb()
        act(qb[:, 0:512], d["a"], AF.Copy, [d["at"]], [qbt])
        d.update(qb=qb, qbt=qbt)

    def R_s2(n):
        col, th, outs = rope_items[n]
        d = R_st[n]
        b, btl = next_bank()
        mm(b, permb[:], d["qb"][:, 0:512], True, True, [T["perm"], d["qbt"]], [btl])
        t1, t1t = next_stg()
        tt("dve", t1[:, 0:512], d["a"], ropeC[:, th * 512:(th + 1) * 512], ALU.mult, [d["at"], T["ropeC"]], [t1t])
        tt("dve", t1[:, 512:1024], b, ropeS[:, th * 512:(th + 1) * 512], ALU.mult, [btl, T["ropeS"]], [t1t])
        for (oap, lo, hi, tiles) in outs:
            tt("dve", oap, t1[lo:hi, 0:512], t1[lo:hi, 512:1024], ALU.add, [t1t], tiles)

    pipeline(len(rope_items), [R_s0, R_s1, R_s2])
    P.alias(T["GLG"], T_modw[0:4])
    for c in range(4):
        for th in range(2):
            a, at = proj_fm(wlx, T["wlx"], C_LG[c], th)
            act(GLG[:, c, th * 512:(th + 1) * 512], a, AF.Gelu, [at], [T["GLG"]])
    chk(3)

    for n in range(9, 12):
        pl = mod_plan[n]
        mod_part(pl[0], pl[1], pl[2], pl[3], T[pl[4]], pl[5])
    P.alias(T["wout"], [T["wqk"]])
    P.dma("pool", wout, wout_d.rearrange("(k p) n -> p k n", p=128), writes=[T["wout"]])
    for nm in ["XC", "XCB", "T1", "T2", "T3", "HS0", "HS1"]:
        P.alias(T[nm], T_modw[0:4])
    for tl in T_lxb:
        P.alias(tl, T_modw[0:4])
    for tl in T_mixa + T_mixl:
        P.alias(tl, [T["gs1"], T["shift1"], T_modw[6]])

    chk(4)
    S_BANKS = [0, 1]
    P_BANK = 2
    X_BANK = [3, 4]
    R_BANK = 5
    L_BANKS = [6, 7]
    lst = {"s": 0, "l": 0}

    def next_sbank():
        b = S_BANKS[lst["s"] % 2]
        lst["s"] += 1
        return bank_ap(b), T_bank[b]

    def next_lbank():
        b = L_BANKS[lst["l"] % 2]
        lst["l"] += 1
        return bank_ap(b), T_bank[b]

    units = []
    for i in range(8):
        for g in range(2):
            blocks = []
            if i > 0:
                blocks.append(("loc", i - 1, 0))
            blocks.append(("loc", i, None))
            if i < 7:
                blocks.append(("loc", i + 1, 1))
            blocks.append(("ctx", 0, None))
            blocks.append(("ctx", 1, None))
            for bi, (kind, j, slot) in enumerate(blocks):
                units.append(dict(i=i, g=g, kind=kind, j=j, slot=slot, first=(bi == 0), last=(bi == len(blocks) - 1)))

    def unit_S(u):
        i, g = u["i"], u["g"]
        sap, stl = next_sbank()
        if u["kind"] == "loc":
            kcol = u["j"] * 128
            ktile = T_kp[u["j"] // 4]
        else:
            kcol = 1024 + u["j"] * 128
            ktile = T["kctx"]
        if u["slot"] is None:
            mm(sap.rearrange("p (c n) -> p c n", c=4), kpad[g][:, kcol:kcol + 128], qT[:, :, i * 128:(i + 1) * 128], True, True,
               [ktile, T_qT[i // 4]], [stl])
        else:
            mm(sap.rearrange("p (c n) -> p c n", c=4), kpad[g][:, kcol:kcol + 128], qT[:, :, i * 128:(i + 1) * 128], True, False,
               [ktile, T_qT[i // 4]], [stl], inc=False)
            m = masks[:, i * 2 + u["slot"], :]
            mm(sap.rearrange("p (c n) -> p c n", c=4), ident[:], m.unsqueeze(1).broadcast_to([128, 4, 128]), False, True,
               [T["ident"], T["masks"]], [stl])
        u["sap"], u["stl"] = sap, stl

    def unit_exp(u):
        i = u["i"]
        pt, ptt = next_pt()
        if u["kind"] == "ctx":
            act(pt, u["sap"], AF.Exp, [u["stl"], T["vecs"]], [ptt], scale=0.125, bias=vecs[:, V_CTXB:V_CTXB + 1])
        else:
            act(pt, u["sap"], AF.Exp, [u["stl"]], [ptt], scale=0.125)
        u["pt"], u["ptt"] = pt, ptt

    def unit_PV(u):
        i, g = u["i"], u["g"]
        xap, xt = bank_ap(X_BANK[g]), T_bank[X_BANK[g]]
        if u["kind"] == "loc":
            vblk, vtile = u["j"], T_va[u["j"]]
        else:
            vblk, vtile = 8 + u["j"], T["vctx"]
        mm(xap, vaug[g][:, vblk, :], u["pt"], u["first"], u["last"], [vtile, u["ptt"]], [xt], inc=u["last"])
        if u["last"]:
            if g == 0:
                tt("dve", rrow[64:65, :], xap[64:65, :], sinkrow[64:65].rearrange("p h n -> p (h n)"), ALU.add, [xt, T["sinkrow"]], [T["rrow0"]])
            else:
                tt("dve", rrow[0:1, :], xap[0:1, :], sinkrow[0:1].rearrange("p h n -> p (h n)"), ALU.add, [xt, T["sinkrow"]], [T["rrow1"]])
            pending.append([1, 1, i, g])

    def norm_stage(stage, i, g):
        xap, xt = bank_ap(X_BANK[g]), T_bank[X_BANK[g]]
        rap, rt = bank_ap(R_BANK), T_bank[R_BANK]
        p = 64 if g == 0 else 0
        lo, hi = (0, 64) if g == 0 else (64, 128)
        rr_t = T["rrow0"] if g == 0 else T["rrow1"]
        rs_t = T["Rsb0"] if g == 0 else T["Rsb1"]
        if stage == 1:
            for c in range(4):
                mm(rap[:, c:c + 1], rrow[p:p + 1, c * 128:(c + 1) * 128], ones[p:p + 1, 0:1], True, True, [rr_t, T["ones"]], [rt], inc=(c == 3))
        elif stage == 2:
            recip(rT[:, :], rap[:, 0:4], [rt], [T_rT])
            cp("dve", bc[:, :, :], rT[:, :].unsqueeze(2).broadcast_to([128, 4, 64]), [T_rT], [T_bc])
        elif stage == 3:
            for c in range(4):
                mm(rap[lo:hi, c * 128:(c + 1) * 128], bc[:, c, :], identf[:, :], True, True, [T_bc, T["identf"]], [rt], inc=(c == 3))
        else:
            cp("dve", Rsb[lo:hi, :], rap[lo:hi, :], [rt], [rs_t])
            tt("dve", mixT[lo:hi, 0:4, i * 128:(i + 1) * 128], xap[lo:hi, :].rearrange("p (c n) -> p c n", c=4),
               Rsb[lo:hi, :].rearrange("p (c n) -> p c n", c=4), ALU.mult, [xt, rs_t], [T_mixa[i * 2 + g]])

    pending = []
    rT = sb("rT", [128, 4], F32)
    bc = sb("bcn", [128, 4, 64], F32)
    T_rT = P.tile("rT")
    T_bc = P.tile("bc")

    def run_pending():
        for p in list(pending):
            p[0] -= 1
            if p[0] <= 0:
                pending.remove(p)
                norm_stage(p[1], p[2], p[3])
                if p[1] < 4:
                    pending.append([1, p[1] + 1, p[2], p[3]])

    def attention_steps():
        n = len(units)
        unit_S(units[0])
        yield
        for k in range(n):
            unit_exp(units[k])
            if k + 1 < n:
                unit_S(units[k + 1])
            run_pending()
            if k >= 1:
                unit_PV(units[k - 1])
            yield
        unit_PV(units[n - 1])
        yield
        for _ in range(6):
            run_pending()
            yield

    T1s = [T1, STG[:, 0, :]]
    T2s = [T2, STG[:, 1, :]]
    T3s = [T3, STG[:, 2, :]]
    XCs = [XC, STG[:, 3, :]]
    XCBs = [XCB, HB[:, 0, :]]
    TT1 = [T["T1"], P.tile("T1b")]
    TT2 = [T["T2"], P.tile("T2b")]
    TT3 = [T["T3"], P.tile("T3b")]
    TXC = [T["XC"], P.tile("XCb")]
    TXCB = [T["XCB"], P.tile("XCBb")]
    for tl in [TT1[1], TT2[1], TT3[1], TXC[1]]:
        P.alias(tl, T_stg)
    P.alias(TXCB[1], T_hb)
    fl = vecs[:, V_FLAG:V_FLAG + 1]

    def lru_prep(c):
        bi = c % 2
        lxb = LXB[bi]
        lxt = T_lxb[bi]
        xc = XCs[bi]
        xct = TXC[bi]

        def lx_mm(th):
            bap, bt = bank_ap(P_BANK), T_bank[P_BANK]
            for k in range(8):
                mm(bap, wlx[:, k, C_LX[c]:C_LX[c] + 128], hT[:, k, th * 512:(th + 1) * 512], k == 0, k == 7,
                   [T["wlx"]] + T_hT[th * 4:th * 4 + 4], [bt], inc=(k == 7))
            return bap, bt

        def lx_cp(b, th):
            act(lxb[:, 2 * th:2 * th + 2, 2:258], b[0].rearrange("p (s n) -> p s n", s=2), AF.Copy, [b[1]], [lxt])

        b0 = lx_mm(0)
        yield
        lx_cp(b0, 0)
        yield
        b1 = lx_mm(1)
        yield
        lx_cp(b1, 1)
        memset("dve", lxb[:, 0:1, 0:2], 0.0, [lxt])
        memset("dve", lxb[:, 3:4, 258:259], 0.0, [lxt])
        ts("dve", lxb[:, 1:4, 0:2], lxb[:, 0:3, 256:258], fl, ALU.mult, [lxt, T["vecs"]], [lxt])
        ts("dve", lxb[:, 0:3, 258:259], lxb[:, 1:4, 2:3], fl, ALU.mult, [lxt, T["vecs"]], [lxt])
        xc3 = xc.rearrange("p (s n) -> p s n", s=4)
        act(xc3, lxb[:, :, 2:258], AF.Identity, [lxt, T["vecs"]], [xct],
            scale=vecs[:, V_CONVW + 2 * 4 + c:V_CONVW + 2 * 4 + c + 1], bias=vecs[:, V_CONVB + c:V_CONVB + c + 1])
        yield
        for j in (0, 1, 3):
            stt(xc3, lxb[:, :, j:j + 256], vecs[:, V_CONVW + j * 4 + c:V_CONVW + j * 4 + c + 1], xc3, ALU.mult, ALU.add,
                [lxt, T["vecs"], xct], [xct])
            yield
        cp("dve", XCBs[bi], xc, [xct], [TXCB[bi]])
        yield

    def lru_rec(c):
        bi = c % 2
        xc, xct = XCs[bi], TXC[bi]
        xcb, xcbt = XCBs[bi], TXCB[bi]
        hold = {}

        def g_mm(d, gi, th):
            b = L_BANKS[d]
            bap, bt = bank_ap(b), T_bank[b]
            mm(bap, gatew[:, (gi * 2 + d) * 4 + c, :], xcb[:, th * 512:(th + 1) * 512], True, True, [T["gatew"], xcbt], [bt])
            hold[d] = (bap, bt)

        def g_tanh(d, dst, dst_t, th, bias_col):
            bap, bt = hold[d]
            act(dst[:, th * 512:(th + 1) * 512], bap, AF.Tanh, [bt, T["vecs"]], [dst_t], scale=0.5, bias=vecs[:, bias_col:bias_col + 1])

        for gi, dsts, dts, bcol in ((0, T1s, TT1, V_HBR), (1, T3s, TT3, V_HBI)):
            for th in range(2):
                for d in range(2):
                    g_mm(d, gi, th)
                yield
                for d in range(2):
                    g_tanh(d, dsts[d], dts[d], th, bcol + d * 4 + c)
                if gi == 1 and th == 1:
                    for d in range(2):
                        vb = d * 4 + c
                        act(T2s[d], T1s[d], AF.Exp, [TT1[d], T["vecs"]], [TT2[d]], scale=vecs[:, V_HC1 + vb:V_HC1 + vb + 1], bias=vecs[:, V_HC1 + vb:V_HC1 + vb + 1])
                yield
        for d in range(2):
            tt("dve", T1s[d], T2s[d], T2s[d], ALU.mult, [TT2[d]], [TT1[d]])
        yield
        for d in range(2):
            act(T1s[d], T1s[d], AF.Sqrt, [TT1[d]], [TT1[d]], scale=-0.25, bias=0.25)
        for d in range(2):
            tt("dve", T1s[d], T1s[d], xc, ALU.mult, [TT1[d], xct], [TT1[d]])
        yield
        for d in range(2):
            stt(T3s[d], T3s[d], 1.0, T1s[d], ALU.add, ALU.mult, [TT3[d], TT1[d]], [TT3[d]])
            a3 = T2s[d].rearrange("p (s n) -> p s n", s=4)
            if d == 0:
                ts("dve", a3[:, 1:4, 0:1], a3[:, 1:4, 0:1], fl, ALU.mult, [TT2[d], T["vecs"]], [TT2[d]])
            else:
                ts("dve", a3[:, 0:3, 255:256], a3[:, 0:3, 255:256], fl, ALU.mult, [TT2[d], T["vecs"]], [TT2[d]])
        yield "tail"
        for d in range(2):
            vb = d * 4 + c
            hs = HS[d]
            hst = T["HS%d" % d]
            h0 = vecs[:, V_H0 + vb:V_H0 + vb + 1]
            fs3 = FS[:, :].rearrange("p (s q) -> p s q", s=4)[:, :, vb:vb + 1]
            if d == 0:
                scan(hs, T2s[d], T3s[d], h0, [TT2[d], TT3[d], T["vecs"]], [hst])
                cp("dve", fs3, hs.rearrange("p (s n) -> p s n", s=4)[:, :, 255:256], [hst], [T["FS"]])
            else:
                scan(hs[:, ::-1], T2s[d][:, ::-1], T3s[d][:, ::-1], h0, [TT2[d], TT3[d], T["vecs"]], [hst])
                cp("dve", fs3, hs.rearrange("p (s n) -> p s n", s=4)[:, :, 0:1], [hst], [T["FS"]])
            yield
        tt("dve", HS[0], HS[0], HS[1], ALU.add, [T["HS0"], T["HS1"]], [T["HS0"]])
        tt("dve", mixT[:, 4 + c, :], HS[0], GLG[:, c, :], ALU.mult, [T["HS0"], T["GLG"]], [T_mixl[c]])
        yield

    def merged(g1, g2):
        d1 = d2 = False
        while not (d1 and d2):
            if not d1:
                try:
                    next(g1)
                    yield
                except StopIteration:
                    d1 = True
            if not d2:
                try:
                    next(g2)
                    yield
                except StopIteration:
                    d2 = True

    def empty():
        return
        yield

    def merged_n(gens):
        live = list(gens)
        while live:
            for g in list(live):
                try:
                    next(g)
                    yield
                except StopIteration:
                    live.remove(g)

    def all_lru():
        for _ in lru_prep(0):
            yield
        recs = {c: lru_rec(c) for c in range(4)}

        def head(c):
            for v in recs[c]:
                if v == "tail":
                    return
                yield

        def tail(c):
            for v in recs[c]:
                yield

        for c in range(4):
            parts = [head(c)]
            if c > 0:
                parts.append(tail(c - 1))
            if c < 3:
                parts.append(lru_prep(c + 1))
            for _ in merged_n(parts):
                yield
        for _ in tail(3):
            yield

    ga = attention_steps()
    gl = all_lru()
    a_done = l_done = False
    RATIO = 1
    while not (a_done and l_done):
        for _ in range(RATIO):
            if not a_done:
                try:
                    next(ga)
                except StopIteration:
                    a_done = True
        if not l_done:
            try:
                next(gl)
            except StopIteration:
                l_done = True

    for tl in T_stg:
        P.alias(tl, TT1 + TT2 + TT3 + TXC)
    for tl in T_hb:
        P.alias(tl, TXCB)

    chk(5)
    bap, bt = next_bank()
    trp(bap[0:32, 0:128], FS[:, :], identf[:], [T["FS"], T["identf"]], [bt])
    FST = HB[:, 1, :].bitcast(F32)[0:32, 0:128]
    act(FST, bap[0:32, 0:128], AF.Copy, [bt], [T_hb[1]])
    P.dma("sp", nst_d, FST, reads=[T_hb[1]], is_output=True)

    xb_tiles = [T[n] for n in ["XC", "XCB", "T1", "T2", "T3", "HS0", "HS1", "kctx", "vctx", "rrow0", "rrow1", "Rsb0", "Rsb1", "GLG"]] \
        + T_lxb + T_kp + T_va + T_pt + T_modw[0:4]
    P.alias(T["wdn"], xb_tiles)
    wdn_src = wdn_d.rearrange("(c p) n -> p c n", p=128)

    chk(6)
    h2T = RC
    T_h2T = [P.tile("h2T%d" % t) for t in range(8)]
    for t in range(8):
        P.alias(T_h2T[t], T_hT)
    EB = [carve(XA, off, 4096, F32) for off in (0, 4096, 8192, 12288, 40960, 45056, 49152, 53248)]
    T_eb = [P.tile("eb%d" % i) for i in range(8)]
    for tl in T_eb:
        P.alias(tl, [T["wlx"], T["ropeC"], T["ropeS"]] + T_qT)
    EJ = carve(XA, 57344, 2048, BF16)
    T_ej = P.tile("ej")
    P.alias(T_ej, [T["masks"]])
    est = {"x": 0, "t": 0, "u": 0, "p": 0, "tb": 0}

    def e_xbuf():
        i = est["x"]
        est["x"] = (i + 1) % 4
        return EB[i], T_eb[i]

    def e_tbuf():
        i = 4 + est["t"]
        est["t"] ^= 1
        return EB[i], T_eb[i]

    def e_ubuf():
        i = 6 + est["u"]
        est["u"] ^= 1
        return EB[i], T_eb[i]

    def e_pair():
        p = est["p"]
        est["p"] = (p + 1) % 3
        return PS[p], [T_bank[2 * p], T_bank[2 * p + 1]]

    def e_tbank():
        b = 6 + est["tb"]
        est["tb"] ^= 1
        return bank_ap(b), T_bank[b]

    E_st = {}
    wup_src = wup_d.rearrange("(k p) n -> p k n", p=128)
    NPIECE = 16

    def load_wup(pc):
        P.dma("pool", WUP[pc % 4], wup_src[:, :, pc * 256:(pc + 1) * 256], writes=[T_wup[pc % 4]])

    def E_s0(t):
        pp, ppt = e_pair()
        for hf in range(2):
            for c in range(8):
                mm(pp[:, hf * 512:(hf + 1) * 512], mixT[:, c, t * 128:(t + 1) * 128], wout[:, c, hf * 512:(hf + 1) * 512], c == 0, c == 7,
                   [T["wout"], T_mixa[2 * t], T_mixa[2 * t + 1]] + T_mixl, [ppt[hf]], inc=(c == 7))
        xt, xtt = e_xbuf()
        P.dma("sp", xt, x_d[t * 128:(t + 1) * 128, :], writes=[xtt])
        E_st[t] = dict(pp=pp, ppt=ppt, xt=xt, xtt=xtt)
        if t == NTT - 1:
            for i in range(4):
                P.alias(T_wup[i], T_mixa + T_mixl)
            for pc in range(3):
                load_wup(pc)

    def E_s1(t):
        d = E_st[t]
        r, rt_ = rstd_from(d["pp"][:, :], d["ppt"], EJ, T_ej)
        d.update(r=r, rt=rt_)

    def E_s2(t):
        d = E_st[t]
        pp, ppt, xt, xtt = d["pp"], d["ppt"], d["xt"], d["xtt"]
        tmp, tmpt = e_tbuf()
        stt(tmp, pp[:, :], d["r"], gg1, ALU.mult, ALU.mult, ppt + [d["rt"], T["gg1"]], [tmpt])
        tt("dve", xt, xt, tmp, ALU.add, [xtt, tmpt], [xtt])
        P.dma("sp", x1s_d[t * 128:(t + 1) * 128, :], xt, reads=[xtt], writes=[T_x1s[t]], key=xtt)

    def E_s3(t):
        d = E_st[t]
        xt, xtt = d["xt"], d["xtt"]
        r2, rt2 = rstd_from(xt, [xtt], EJ, T_ej)
        tmp2, tmp2t = e_ubuf()
        act(tmp2, xt, AF.Identity, [xtt, rt2], [tmp2t], scale=r2)
        d.update(tmp2=tmp2, tmp2t=tmp2t)

    def E_s4(t):
        d = E_st[t]
        tmp2, tmp2t = d["tmp2"], d["tmp2t"]
        hb, hbt = next_hb()
        tt("dve", tmp2, tmp2, gs2, ALU.mult, [tmp2t, T["gs2"]], [tmp2t])
        tt("dve", hb, tmp2, shift2, ALU.add, [tmp2t, T["shift2"]], [hbt])
        d.update(hb=hb, hbt=hbt)

    def E_s5(t):
        d = E_st[t]
        bap, bt = e_tbank()
        bb = bap.bitcast(BF16)
        for j in range(8):
            trp(bb[:, j * 128:(j + 1) * 128], d["hb"][:, j * 128:(j + 1) * 128], ident[:], [d["hbt"], T["ident"]], [bt], inc=(j == 7))
        d.update(bb=bb, bt=bt)

    def E_s6(t):
        d = E_st[t]
        act(h2T[:, :, t * 128:(t + 1) * 128], d["bb"].rearrange("p (k n) -> p k n", k=8), AF.Copy, [d["bt"]], [T_h2T[t]])

    pipeline(NTT, [E_s0, E_s1, E_s2, E_s3, E_s4, E_s5, E_s6])
    st["bank"] = 0
    st["pair"] = 0

    chk(7)
    for c in range(32):
        P.alias(T_act[c], [T["wlx"], T["wout"], T["wqk"], T["ropeC"], T["ropeS"], T["masks"], T["sinkrow"]] + T_qT + T_eb + [T_ej])
    for pc in range(NPIECE):
        if pc + 3 < NPIECE:
            load_wup(pc + 3)
        if pc % 2 == 1:
            q = pc // 2
            P.dma("pool", wdn[:, q * 4:(q + 1) * 4, :], wdn_src[:, q * 4:(q + 1) * 4, :], writes=[T["wdn"]])
        for cc in range(2):
            c = pc * 2 + cc
            for th in range(2):
                bap, bt = next_bank()
                for k in range(8):
                    mm(bap, WUP[pc % 4][:, k, cc * 128:(cc + 1) * 128], h2T[:, k, th * 512:(th + 1) * 512], k == 0, k == 7,
                       [T_wup[pc % 4]] + T_h2T[th * 4:th * 4 + 4], [bt], inc=(k == 7))
                rl, rlt = next_stg()
                act(rl[:, 0:512], bap, AF.Relu, [bt], [rlt])
                tt("dve", actT[:, c, th * 512:(th + 1) * 512], rl[:, 0:512], rl[:, 0:512], ALU.mult, [rlt], [T_act[c]])

    chk(8)
    G_st = {}

    def G_s0(t):
        pp, ppt = next_pair()
        for hf in range(2):
            for c in range(32):
                mm(pp[:, hf * 512:(hf + 1) * 512], actT[:, c, t * 128:(t + 1) * 128], wdn[:, c, hf * 512:(hf + 1) * 512], c == 0, c == 31,
                   [T["wdn"], T_act[c]], [ppt[hf]], inc=(c == 31))
        xt, xtt = next_stg()
        P.dma("sp", xt, x1s_d[t * 128:(t + 1) * 128, :], reads=[T_x1s[t]], writes=[xtt])
        G_st[t] = (pp, ppt, xt, xtt)

    def G_s1(t):
        pp, ppt, xt, xtt = G_st[t]
        tmp, tmpt = next_stg()
        r, rt_ = rstd_from(pp[:, :], ppt, tmp.bitcast(BF16)[:, 0:1024], tmpt)
        stt(tmp, pp[:, :], r, gg2, ALU.mult, ALU.mult, ppt + [rt_, T["gg2"]], [tmpt])
        tt("dve", xt, xt, tmp, ALU.add, [xtt, tmpt], [xtt])
        P.dma("sp", y_d[t * 128:(t + 1) * 128, :], xt, reads=[xtt], is_output=True)

    pipeline(NTT, [G_s0, G_s1])


N_Q_HEADS = 8
HEAD_DIM = 64


def _host_layout(inputs):
    f32 = np.float32
    w_in = np.asarray(inputs["w_in"][0], f32)
    q_cols = []
    for c in range(4):
        q_cols += list(range(c * 64, c * 64 + 64)) + list(range((4 + c) * 64, (4 + c) * 64 + 64))
    q_cols = np.array(q_cols)

    def partner(cols):
        base = (cols // 32) * 32
        return base + ((cols % 32) + 16) % 32

    k_cols = np.arange(512, 640)
    kv_cols = np.arange(512, 768)
    lx_cols = np.arange(768, 1280)
    lg_cols = np.arange(1280, 1792)
    ext = []
    for c in range(4):
        ext += list(lx_cols[c * 128:(c + 1) * 128]) + list(lg_cols[c * 128:(c + 1) * 128])
    ext += list(kv_cols) + list(q_cols) + list(k_cols)
    ext = np.array(ext)
    assert ext.shape[0] == NEXT
    w_in_ext = np.ascontiguousarray(w_in[:, ext])
    w_out = np.asarray(inputs["w_out"][0], f32)
    rows = list(q_cols) + list(range(512, 1024))
    w_out_p = np.ascontiguousarray(w_out[rows, :])

    gvec = np.stack([inputs["g_pre_mix"][0], inputs["g_post_mix"][0], inputs["g_pre_mlp"][0], inputs["g_post_mlp"][0]]).astype(f32)

    def pl(v):
        return np.asarray(v, f32).reshape(4, 128).T

    conv_w = inputs["conv_w"][0]
    base = np.zeros((128, NVEC), f32)
    for j in range(4):
        base[:, V_CONVW + j * 4:V_CONVW + j * 4 + 4] = pl(conv_w[j])
    base[:, V_CONVB:V_CONVB + 4] = pl(inputs["conv_b"][0])
    for d in range(2):
        base[:, V_BR + d * 4:V_BR + d * 4 + 4] = pl(inputs["lru_b_r"][0, d])
        base[:, V_BI + d * 4:V_BI + d * 4 + 4] = pl(inputs["lru_b_i"][0, d])
        base[:, V_LAM + d * 4:V_LAM + d * 4 + 4] = pl(inputs["lru_lam"][0, d])

    gatew = np.zeros((128, 16, 128), f32)
    for gi, nm in enumerate(["lru_w_r", "lru_w_i"]):
        w = np.asarray(inputs[nm][0], f32)
        for d in range(2):
            for c in range(4):
                for b in range(2):
                    gatew[b * 64:(b + 1) * 64, (gi * 2 + d) * 4 + c, b * 64:(b + 1) * 64] = w[d, c * 2 + b]
    gatew = gatew.reshape(128, 16 * 128)

    sink = np.asarray(inputs["attn_sink"][0], f32)
    sinkmat = np.zeros((128, 4), f32)
    sinkmat[64, :] = sink[0:4]
    sinkmat[0, :] = sink[4:8]
    sel = np.zeros((128, 128), f32)
    sel[64, 0:64] = 1.0
    sel[0, 64:128] = 1.0
    ident = np.eye(128, dtype=f32)
    ff = np.arange(128)
    perm = np.zeros((128, 128), f32)
    perm[partner(ff), ff] = 1.0

    f = np.arange(128)
    dd = f % 64
    half = dd // 32
    jj = dd % 32
    fi = jj % 16
    first = jj < 16
    inv = (10000.0 ** (-np.arange(16, dtype=np.float32) / 16)).astype(f32)
    tt = np.arange(NT)
    rowp = (tt // 64).astype(f32)
    colp = (tt % 64).astype(f32)
    pos = np.where(half[:, None] == 0, rowp[None, :], colp[None, :]).astype(f32)
    ang = (pos * inv[fi][:, None]).astype(f32)
    rc_s = np.cos(ang).astype(f32)
    rs_s = (np.sin(ang) * np.where(first, -1.0, 1.0)[:, None]).astype(f32)
    rc_p = np.ones((128, NT), f32)
    rs_p = np.zeros((128, NT), f32)

    b = np.arange(128)[:, None]
    a = np.arange(128)[None, :]
    m_s = np.zeros((128, 16, 128), f32)
    m_p = np.zeros((128, 16, 128), f32)
    for i in range(8):
        m_s[:, i * 2 + 0, :] = (b >= a)
        m_s[:, i * 2 + 1, :] = (b <= a)
        m_p[:, i * 2 + 0, :] = 1.0 if (i % 2 == 1) else 0.0
        m_p[:, i * 2 + 1, :] = 1.0 if (i % 2 == 0) else 0.0
    m_s = ((1.0 - m_s) * -240000.0).astype(f32).reshape(128, 2048)
    m_p = ((1.0 - m_p) * -240000.0).astype(f32).reshape(128, 2048)

    shared = {
        "mod_w": np.ascontiguousarray(np.asarray(inputs["mod_w"][0], f32)),
        "mod_b": np.asarray(inputs["mod_b"][0], f32).reshape(1, 6 * D),
        "gvec": gvec, "w_in_ext": w_in_ext, "w_out_p": w_out_p,
        "w_up": np.ascontiguousarray(np.asarray(inputs["mlp_w_up"][0], f32)),
        "w_down": np.ascontiguousarray(np.asarray(inputs["mlp_w_down"][0], f32)),
        "sinkmat": sinkmat, "ident": ident, "gatew": gatew, "perm": perm,
    }
    in_maps = []
    xp = np.asarray(inputs["x_prompt"], f32)
    xs = np.asarray(inputs["x_sample"], f32)
    for core in range(8):
        m = dict(shared)
        vec = base.copy()
        if core < 4:
            m["x"] = np.ascontiguousarray(xp[core * 4:(core + 1) * 4].reshape(NT, D))
            cv = np.asarray(inputs["c_ctx"], f32)
            m["ck"] = np.ascontiguousarray(np.asarray(inputs["cache_k"][0, 0], f32).reshape(256, 128))
            m["cv"] = np.ascontiguousarray(np.asarray(inputs["cache_v"][0, 0], f32).reshape(256, 128))
            vec[:, V_FLAG] = 0.0
            vec[:, V_CTXB] = -30000.0
            m["rope_c"], m["rope_s"], m["masks"] = rc_p, rs_p, m_p
        else:
            bi = core - 4
            m["x"] = np.ascontiguousarray(xs[bi])
            cv = np.asarray(inputs["c"][bi], f32)
            m["ck"] = np.ascontiguousarray(np.asarray(inputs["cache_k"][bi, 0], f32).reshape(256, 128))
            m["cv"] = np.ascontiguousarray(np.asarray(inputs["cache_v"][bi, 0], f32).reshape(256, 128))
            vec[:, V_FLAG] = 1.0
            vec[:, V_CTXB] = 0.0
            for d in range(2):
                vec[:, V_H0 + d * 4:V_H0 + d * 4 + 4] = pl(inputs["state_lru"][bi, 0, d])
            m["rope_c"], m["rope_s"], m["masks"] = rc_s, rs_s, m_s
        m["cvecT"] = np.ascontiguousarray(cv.reshape(8, 128).T)
        m["vecs"] = vec
        in_maps.append(m)
    return in_maps


_CACHE = {}


def kernel(**inputs):
    in_maps = _host_layout(inputs)
    if "nc" not in _CACHE:
        _CACHE["nc"] = build()
    nc = _CACHE["nc"]
    res = run_bass_kernel_spmd(nc, in_maps, core_ids=list(range(8)))
    r = res.results
    y_prompt = np.stack([r[c]["y"] for c in range(4)]).reshape(16, 256, D).astype(np.float32)
    y_sample = np.stack([r[4 + c]["y"] for c in range(4)]).reshape(4, 1024, D).astype(np.float32)
    new_k = np.stack([r[c]["new_k"] for c in range(4)]).reshape(16, 1, 256, 2, 64).astype(np.float32)
    new_v = np.stack([r[c]["new_v"] for c in range(4)]).reshape(16, 1, 256, 2, 64).astype(np.float32)
    ns = np.stack([r[c]["new_state_t"].reshape(4, 2, 4 * 128) for c in range(4)]).reshape(16, 1, 2, 512).astype(np.float32)
    return (y_prompt, y_sample, new_k, new_v, ns)
```
